# Optimizing a Trainium2 kernel written in Bass

```python
import math
import jax, jax.numpy as jnp
from jax import lax
import numpy as np

D_MODEL = 2048
BATCH = 4
SEQ = 8192
DEPTH = 4

W_A = 512
CONV_A = 3
W_B = 512
CONV_B = 31
W_C = 512
GMLP_CHUNK = 128
GMLP_GROUPS = 4
N_HEADS = 4
HEAD_DIM = 128
W_D = N_HEADS * HEAD_DIM
MOBA_BLOCK = 256
MOBA_TOPK = 3
Q_CHUNK = 64
N_BUCKETS = 32
REL_MAX_DIST = 2048
D_FF = 4 * D_MODEL
N_BRANCH = 4
EPS = 1e-6
NEG_INF = -1e30

OFF_A = 0
OFF_B = OFF_A + 3 * W_A
OFF_C = OFF_B + 2 * W_B
OFF_D = OFF_C + 2 * W_C
OFF_G = OFF_D + 3 * W_D
IN_COLS = OFF_G + N_BRANCH * D_MODEL

kernel_name = "hybrid_gated_conv_gmlp_moba_trunk"


def _rms_norm(x, g):
    xf = x.astype(jnp.float32)
    y = xf * lax.rsqrt(jnp.mean(xf * xf, axis=-1, keepdims=True) + EPS)
    return (y * g.astype(jnp.float32)).astype(x.dtype)


def _layer_norm(x, g, b):
    xf = x.astype(jnp.float32)
    mu = jnp.mean(xf, axis=-1, keepdims=True)
    xc = xf - mu
    y = xc * lax.rsqrt(jnp.mean(xc * xc, axis=-1, keepdims=True) + EPS)
    return (y * g.astype(jnp.float32) + b.astype(jnp.float32)).astype(x.dtype)


def _causal_depthwise_conv(x, w):
    k = w.shape[0]
    return lax.conv_general_dilated(
        x, w[:, None, :].astype(x.dtype), window_strides=(1,), padding=[(k - 1, 0)],
        dimension_numbers=("NWC", "WIO", "NWC"), feature_group_count=x.shape[-1])


def _rel_bucket(dist):
    n = jnp.maximum(dist, 0)
    max_exact = N_BUCKETS // 2
    nf = jnp.maximum(n, 1).astype(jnp.float32)
    large = max_exact + (jnp.log(nf / max_exact) / math.log(REL_MAX_DIST / max_exact)
                         * (N_BUCKETS - max_exact)).astype(jnp.int32)
    large = jnp.minimum(large, N_BUCKETS - 1)
    return jnp.where(n < max_exact, n, large)


def _moba_attention(q, k, v, rel_bias):
    bsz, seq = q.shape[0], q.shape[1]
    n_blk = -(-seq // MOBA_BLOCK)
    s_pad = n_blk * MOBA_BLOCK
    pad = ((0, 0), (0, s_pad - seq), (0, 0), (0, 0))
    q = jnp.pad(q, pad).transpose(0, 2, 1, 3)
    k = jnp.pad(k, pad).transpose(0, 2, 1, 3)
    v = jnp.pad(v, pad).transpose(0, 2, 1, 3)
    k_blk = k.reshape(bsz, N_HEADS, n_blk, MOBA_BLOCK, HEAD_DIM)
    v_blk = v.reshape(bsz, N_HEADS, n_blk, MOBA_BLOCK, HEAD_DIM)
    k_mean = jnp.mean(k_blk.astype(jnp.float32), axis=3)
    pos = jnp.arange(s_pad, dtype=jnp.int32)
    q_blk_id = pos // MOBA_BLOCK
    gate = jnp.einsum("bhsd,bhnd->bhsn", q.astype(jnp.float32), k_mean)
    past = jnp.arange(n_blk, dtype=jnp.int32)[None, :] < q_blk_id[:, None]
    gate = jnp.where(past, gate, NEG_INF)
    n_sel = min(MOBA_TOPK, n_blk)
    _, sel_idx = lax.top_k(gate, n_sel)
    sel_valid = sel_idx < q_blk_id[:, None]

    n_chunk = s_pad // Q_CHUNK
    b_idx = jnp.arange(bsz)[:, None, None, None]
    h_idx = jnp.arange(N_HEADS)[None, :, None, None]
    h5 = jnp.arange(N_HEADS)[None, :, None, None, None]
    offs = jnp.arange(MOBA_BLOCK, dtype=jnp.int32)
    scale = HEAD_DIM ** -0.5

    def chunk(c):
        start = c * Q_CHUNK
        q_c = lax.dynamic_slice_in_dim(q, start, Q_CHUNK, axis=2)
        idx_c = lax.dynamic_slice_in_dim(sel_idx, start, Q_CHUNK, axis=2)
        val_c = lax.dynamic_slice_in_dim(sel_valid, start, Q_CHUNK, axis=2)
        q_pos = start + jnp.arange(Q_CHUNK, dtype=jnp.int32)
        own = start // MOBA_BLOCK
        k_own = lax.dynamic_index_in_dim(k_blk, own, axis=2, keepdims=False)
        v_own = lax.dynamic_index_in_dim(v_blk, own, axis=2, keepdims=False)
        d_own = q_pos[:, None] - (own * MOBA_BLOCK + offs)[None, :]
        l_own = (jnp.einsum("bhqd,bhkd->bhqk", q_c, k_own).astype(jnp.float32) * scale
                 + rel_bias[_rel_bucket(d_own)].transpose(2, 0, 1).astype(jnp.float32))
        l_own = jnp.where(d_own >= 0, l_own, NEG_INF)
        k_g = k_blk[b_idx, h_idx, idx_c]
        v_g = v_blk[b_idx, h_idx, idx_c]
        d_sel = q_pos[None, None, :, None, None] - (idx_c[..., None] * MOBA_BLOCK + offs)
        l_sel = (jnp.einsum("bhqd,bhqnkd->bhqnk", q_c, k_g).astype(jnp.float32) * scale
                 + rel_bias[_rel_bucket(d_sel), h5].astype(jnp.float32))
        l_sel = jnp.where(val_c[..., None], l_sel, NEG_INF)
        logits = jnp.concatenate(
            [l_own, l_sel.reshape(bsz, N_HEADS, Q_CHUNK, n_sel * MOBA_BLOCK)], axis=-1)
        p = jax.nn.softmax(logits, axis=-1).astype(v.dtype)
        p_own = p[..., :MOBA_BLOCK]
        p_sel = p[..., MOBA_BLOCK:].reshape(bsz, N_HEADS, Q_CHUNK, n_sel, MOBA_BLOCK)
        return (jnp.einsum("bhqk,bhkd->bhqd", p_own, v_own)
                + jnp.einsum("bhqnk,bhqnkd->bhqd", p_sel, v_g))

    out = lax.map(chunk, jnp.arange(n_chunk, dtype=jnp.int32))
    out = out.transpose(1, 0, 3, 2, 4).reshape(bsz, s_pad, W_D)
    return out[:, :seq]


def setup_inputs(seed: int = 0) -> dict:
    key = jax.random.key(seed)
    ks = jax.random.split(key, 32)
    f32 = jnp.float32
    nrm = lambda k, shape, s: jax.random.normal(k, shape, f32) * s
    return {
        "x": nrm(ks[0], (BATCH, SEQ, D_MODEL), 1.0),
        "rel_bias": nrm(ks[1], (N_BUCKETS, N_HEADS), 0.5),
        "norm_mix_g": 1.0 + nrm(ks[2], (DEPTH, D_MODEL), 0.05),
        "w_in": nrm(ks[3], (DEPTH, D_MODEL, IN_COLS), D_MODEL ** -0.5),
        "conv_a_w": nrm(ks[4], (DEPTH, CONV_A, W_A), CONV_A ** -0.5),
        "w_out_a": nrm(ks[5], (DEPTH, W_A, D_MODEL), W_A ** -0.5),
        "conv_b_w": nrm(ks[6], (DEPTH, CONV_B, W_B), CONV_B ** -0.5),
        "conv_b_bias": nrm(ks[7], (DEPTH, W_B), 0.02),
        "ln_b_g": 1.0 + nrm(ks[8], (DEPTH, W_B), 0.05),
        "ln_b_b": nrm(ks[9], (DEPTH, W_B), 0.02),
        "w_out_b": nrm(ks[10], (DEPTH, W_B, D_MODEL), W_B ** -0.5),
        "ln_c_g": 1.0 + nrm(ks[11], (DEPTH, W_C), 0.05),
        "ln_c_b": nrm(ks[12], (DEPTH, W_C), 0.02),
        "w_spatial": nrm(ks[13], (DEPTH, GMLP_GROUPS, GMLP_CHUNK, GMLP_CHUNK), GMLP_CHUNK ** -0.5),
        "b_spatial": 1.0 + nrm(ks[14], (DEPTH, GMLP_GROUPS, GMLP_CHUNK), 0.1),
        "w_out_c": nrm(ks[15], (DEPTH, W_C, D_MODEL), W_C ** -0.5),
        "q_norm_g": 1.0 + nrm(ks[16], (DEPTH, HEAD_DIM), 0.05),
        "k_norm_g": 1.0 + nrm(ks[17], (DEPTH, HEAD_DIM), 0.05),
        "w_out_d": nrm(ks[18], (DEPTH, W_D, D_MODEL), W_D ** -0.5),
        "w_o": nrm(ks[19], (DEPTH, D_MODEL, D_MODEL), D_MODEL ** -0.5),
        "norm_mlp_g": 1.0 + nrm(ks[20], (DEPTH, D_MODEL), 0.05),
        "w_mlp_in": nrm(ks[21], (DEPTH, D_MODEL, D_FF), D_MODEL ** -0.5),
        "w_mlp_out": nrm(ks[22], (DEPTH, D_FF, D_MODEL), D_FF ** -0.5),
    }


def reference(x, rel_bias, norm_mix_g, w_in, conv_a_w, w_out_a, conv_b_w, conv_b_bias,
              ln_b_g, ln_b_b, w_out_b, ln_c_g, ln_c_b, w_spatial, b_spatial, w_out_c,
              q_norm_g, k_norm_g, w_out_d, w_o, norm_mlp_g, w_mlp_in, w_mlp_out):
    bsz, seq = x.shape[0], x.shape[1]
    n_chunks = seq // GMLP_CHUNK
    causal_tri = jnp.tril(jnp.ones((GMLP_CHUNK, GMLP_CHUNK), x.dtype))
    for l in range(DEPTH):
        h = _rms_norm(x, norm_mix_g[l])
        z = h @ w_in[l]

        a_b, a_c, a_x = jnp.split(z[..., OFF_A:OFF_B], 3, axis=-1)
        y_a = (a_b * _causal_depthwise_conv(a_c * a_x, conv_a_w[l])) @ w_out_a[l]

        b_a, b_g = jnp.split(z[..., OFF_B:OFF_C], 2, axis=-1)
        hb = _causal_depthwise_conv(b_a * jax.nn.sigmoid(b_g), conv_b_w[l]) + conv_b_bias[l]
        y_b = jax.nn.silu(_layer_norm(hb, ln_b_g[l], ln_b_b[l])) @ w_out_b[l]

        u, vv = jnp.split(jax.nn.gelu(z[..., OFF_C:OFF_D]), 2, axis=-1)
        vv = _layer_norm(vv, ln_c_g[l], ln_c_b[l])
        vv = vv.reshape(bsz, n_chunks, GMLP_CHUNK, GMLP_GROUPS, W_C // GMLP_GROUPS)
        sv = (jnp.einsum("gts,bnsgc->bntgc", w_spatial[l] * causal_tri, vv)
              + b_spatial[l].T[:, :, None])
        y_c = (u * sv.reshape(bsz, seq, W_C)) @ w_out_c[l]

        q, k, v = jnp.split(z[..., OFF_D:OFF_G], 3, axis=-1)
        q = _rms_norm(q.reshape(bsz, seq, N_HEADS, HEAD_DIM), q_norm_g[l])
        k = _rms_norm(k.reshape(bsz, seq, N_HEADS, HEAD_DIM), k_norm_g[l])
        v = v.reshape(bsz, seq, N_HEADS, HEAD_DIM)
        y_d = _moba_attention(q, k, v, rel_bias) @ w_out_d[l]

        g = jax.nn.sigmoid(z[..., OFF_G:].astype(jnp.float32)).astype(x.dtype)
        g = g.reshape(bsz, seq, N_BRANCH, D_MODEL)
        merged = g[:, :, 0] * y_a + g[:, :, 1] * y_b + g[:, :, 2] * y_c + g[:, :, 3] * y_d
        x = x + merged @ w_o[l]

        h2 = _rms_norm(x, norm_mlp_g[l])
        x = x + jnp.square(jax.nn.relu(h2 @ w_mlp_in[l])) @ w_mlp_out[l]
    return x
```

```python
import math
import numpy as np
from contextlib import ExitStack
import concourse.bass as bass
import concourse.mybir as mybir
from concourse.bass_utils import run_bass_kernel_spmd

F32 = mybir.dt.float32
BF16 = mybir.dt.bfloat16
AF = mybir.ActivationFunctionType
ALU = mybir.AluOpType
AX = mybir.AxisListType

D = 2048
SEQ = 8192
BATCH = 4
L = 4
T = 512
OFF_B, OFF_C, OFF_D, OFF_G, IN_COLS = 1536, 2560, 3584, 5120, 13312
DFF = 8192
EPS = 1e-6
NEG = -1e30
SCALE = 128 ** -0.5
RM = 2944
LT = 3072
PV_PER_L = 190
PV_ROWS = 768
PV_NM, PV_NF, PV_CAW, PV_CBW, PV_CBB, PV_LBG, PV_LBB, PV_LCG, PV_LCB, PV_QG, PV_KG = (
    0, 16, 32, 44, 168, 172, 176, 180, 184, 188, 189)


class Obj:
    __slots__ = ("w", "r")

    def __init__(self):
        self.w = None
        self.r = {}


class Trk:
    def __init__(self, nc, es):
        self.nc = nc
        self.es = es
        self.eng = {"pe": nc.tensor, "act": nc.scalar, "dve": nc.vector, "pool": nc.gpsimd, "sp": nc.sync}
        self.sem = {}
        self.cnt = {}
        self.isdma = {}
        self.waited = {e: {} for e in self.eng}
        for e in ("pe", "act", "dve", "pool"):
            self.newsem(e, False)
        self.n_inst = 0

    def newsem(self, key, dma=True):
        self.sem[key] = self.es.enter_context(self.nc.semaphore("s_" + key))
        self.cnt[key] = 0
        self.isdma[key] = dma
        return key

    def _wait(self, e, need):
        for key, val in need.items():
            if key == "pe" and e == "pe":
                continue
            if self.isdma[key]:
                val = self.cnt[key]
            if self.waited[e].get(key, 0) >= val:
                continue
            self.eng[e].wait_ge(self.sem[key], val)
            self.waited[e][key] = val

    def emit(self, e, fn, reads=(), writes=(), dsem=None, serial=False):
        need = {}
        for o in reads:
            if o.w is not None:
                k, v = o.w
                if need.get(k, 0) < v:
                    need[k] = v
        for o in writes:
            if o.w is not None:
                k, v = o.w
                if need.get(k, 0) < v:
                    need[k] = v
            for k, v in o.r.items():
                if need.get(k, 0) < v:
                    need[k] = v
        self._wait(e, need)
        inst = fn()
        key = dsem if dsem is not None else e
        amt = 16 if dsem is not None else 1
        self.cnt[key] += amt
        inst.then_inc(self.sem[key], amt)
        ev = (key, self.cnt[key])
        for o in reads:
            if o.r.get(key, 0) < ev[1]:
                o.r[key] = ev[1]
        for o in writes:
            o.w = ev
            o.r = {}
        self.n_inst += 1
        if serial:
            self.eng[e].wait_ge(self.sem[key], self.cnt[key])
            self.waited[e][key] = self.cnt[key]
        return ev

    def mm(self, mms, reads, writes):
        need = {}
        for o in reads:
            if o.w is not None:
                k, v = o.w
                if need.get(k, 0) < v:
                    need[k] = v
        for o in writes:
            if o.w is not None:
                k, v = o.w
                if need.get(k, 0) < v:
                    need[k] = v
            for k, v in o.r.items():
                if need.get(k, 0) < v:
                    need[k] = v
        self._wait("pe", need)
        inst = None
        for (out, lhsT, rhs, st, sp) in mms:
            inst = self.nc.tensor.matmul(out, lhsT=lhsT, rhs=rhs, start=st, stop=sp)
        self.cnt["pe"] += 1
        inst.then_inc(self.sem["pe"], 1)
        ev = ("pe", self.cnt["pe"])
        for o in reads:
            o.r["pe"] = ev[1]
        for o in writes:
            o.w = ev
            o.r = {}
        self.n_inst += len(mms)
        return ev

    def tr(self, out, in_, ident, reads, writes):
        need = {}
        for o in reads:
            if o.w is not None:
                k, v = o.w
                need[k] = max(need.get(k, 0), v)
        for o in writes:
            if o.w is not None:
                k, v = o.w
                need[k] = max(need.get(k, 0), v)
            for k, v in o.r.items():
                need[k] = max(need.get(k, 0), v)
        self._wait("pe", need)
        inst = self.nc.tensor.transpose(out, in_, ident)
        self.cnt["pe"] += 1
        inst.then_inc(self.sem["pe"], 1)
        ev = ("pe", self.cnt["pe"])
        for o in reads:
            o.r["pe"] = ev[1]
        for o in writes:
            o.w = ev
            o.r = {}
        self.n_inst += 1


def seq_group(pairs_out, lst):
    n = len(lst)
    return [(pairs_out, a, b, i == 0, i == n - 1) for i, (a, b) in enumerate(lst)]


def build(n_tiles=16, depth=L, s_len=SEQ):
    nc = bass.Bass("TRN2", target_bir_lowering=False)
    NKT = s_len // 128

    def din(name, shape):
        return nc.dram_tensor(name, list(shape), F32, kind="ExternalInput").ap()

    x_d = din("x", [s_len, D])
    relb_d = din("rel_bias", [32, 4])
    w_in_d = din("w_in", [L, D, IN_COLS])
    w_oa_d = [din("w_out_" + c, [L, 512, D]) for c in "abcd"]
    w_o_d = din("w_o", [L, D, D])
    w_1_d = din("w_mlp_in", [L, D, DFF])
    w_2_d = din("w_mlp_out", [L, DFF, D])
    wsp_d = din("w_spatial", [L, 4, 128, 128])
    pv_d = din("pv", [PV_ROWS, 128])
    bsp_d = din("bsp", [1, L * 4 * 128])
    oh_d = din("c_oh", [128, LT])
    ohfar_d = din("c_ohfar", [128, 128])
    ident_d = din("c_ident", [128, 128])
    jflip_d = din("c_jflip", [128, 128])
    esel_d = din("c_esel", [32, 4096])
    tril_d = din("c_tril", [128, 128])
    y_d = nc.dram_tensor("y", [s_len, D], F32, kind="ExternalOutput").ap()

    def dint(name, shape, dt=BF16):
        return nc.dram_tensor(name, list(shape), dt, kind="Internal").ap()

    wb_in = dint("wb_in", [L, D, IN_COLS])
    wb_oa = [dint("wb_o" + c, [L, 512, D]) for c in "abcd"]
    wb_o = dint("wb_o", [L, D, D])
    wb_1 = dint("wb_1", [L, D, DFF])
    wb_2 = dint("wb_2", [L, DFF, D])
    kc_d = dint("kcache", [L, 4, 128, s_len])
    vc_d = dint("vcache", [L, 4, 128, NKT, 128])
    ttab_d = dint("ttab", [4, LT + 128], F32)
    rtab_d = dint("rtab", [4, 128, RM])

    es = ExitStack()
    with es:
        tk = Trk(nc, es)

        def sb(name, shape, dt):
            return es.enter_context(nc.sbuf_tensor(name, list(shape), dt))

        def objs(n):
            return [Obj() for _ in range(n)]

        xT = sb("xT", [128, 16, T], F32); o_x = objs(16)
        hT = sb("hT", [128, 16, T], BF16); o_h = objs(16)
        wbuf = [sb("wbuf%d" % i, [128, 8192], BF16) for i in range(2)]; o_wb = objs(2)
        wsm = [sb("wsm%d" % i, [128, 4, 512], BF16) for i in range(2)]; o_ws = objs(2)
        Rb = [sb("Rb%d" % i, [128, RM], BF16) for i in range(2)]; o_Rb = objs(2)
        regX = sb("regX", [128, 16384], BF16); cX = objs(32)
        regY = sb("regY", [128, 8192], BF16); cY = objs(16)
        regZ = sb("regZ", [128, 8704], BF16); cZ = objs(17)
        tmp = [sb("tmp%d" % i, [128, T], F32) for i in range(4)]; o_tmp = objs(4)
        pvt = sb("pvt", [128, PV_ROWS], F32); o_pv = Obj()
        ident_f = sb("ident_f", [128, 128], F32)
        ident_b = sb("ident_b", [128, 128], BF16)
        ones_b = sb("ones_b", [128, 128], BF16)
        ones_f = sb("ones_f", [128, 128], F32)
        onesrow_b = sb("onesrow_b", [128, 128], BF16)
        jflip_b = sb("jflip_b", [128, 128], BF16)
        esel_b = sb("esel_b", [128, 4096], BF16)
        bsp_b = sb("bsp_b", [128, L * 512], BF16)
        wmT = sb("wmT", [128, L * 4, 128], BF16)
        kmean = sb("kmean", [128, L, 4, 64], F32); o_km = Obj()
        haloA = sb("haloA", [128, L, 4, 2], F32); o_hA = Obj()
        haloB = sb("haloB", [128, L, 4, 30], F32); o_hB = Obj()
        selT = sb("selT", [128, T], BF16); o_selT = Obj()
        gatebuf = sb("gatebuf", [128, 4, 64], F32); o_gb = Obj()
        selb = sb("selb", [128, 4, 64], BF16); o_selb = Obj()
        m8 = sb("m8", [128, 4, 8], F32); o_m8 = Obj()
        thr = sb("thr", [128, 4], F32); o_thr = Obj()
        biasfar = sb("biasfar", [128, 4], F32)
        relb = sb("relb", [128, 4], F32)
        relb_s = sb("relb_s", [128, 4], F32)
        vT = sb("vT", [128, 4, 128], BF16); o_vT = Obj()
        o_const = Obj()

        ps = [es.enter_context(nc.psum_tensor("ps%d" % i, [128, 512], F32)) for i in range(8)]
        o_ps = objs(8)
        rot = [0]

        def psnext():
            i = rot[0]
            rot[0] = (i + 1) % 4
            return i

        zc = [regX[:, 0:4096].bitcast(F32).rearrange("p (k c) -> p k c", c=T),
              regX[:, 4096:8192].bitcast(F32).rearrange("p (k c) -> p k c", c=T)]

        def o_zc(z, k):
            return cX[z * 8 + 2 * k: z * 8 + 2 * k + 2]
        yin = regX[:, 8192:16384].rearrange("p (k c) -> p k c", c=T)

        def o_yin(c):
            return [cX[16 + c]]
        hid = regX[:, :].rearrange("p (k c) -> p k c", c=T)

        def o_hid(c):
            return [cX[c]]
        kbuf = [regY[:, 0:2048], regY[:, 2048:4096]]
        vbuf = [regY[:, 4096:6144].rearrange("p (k c) -> p k c", c=128),
                regY[:, 6144:8192].rearrange("p (k c) -> p k c", c=128)]

        def o_kb(s):
            return cY[4 * s:4 * s + 4]

        def o_vb(s):
            return cY[8 + 4 * s:8 + 4 * s + 4]
        mrg = regY[:, :].rearrange("p (k c) -> p k c", c=T)

        def o_mrg(c):
            return [cY[c]]
        bufA = regZ[:, 0:4112].bitcast(F32).rearrange("p (k c) -> p k c", c=514)
        o_bufA = cZ[0:9]
        bufB = regZ[:, 4112:8448].bitcast(F32).rearrange("p (k c) -> p k c", c=542)
        o_bufB = cZ[8:17]
        qn_b = regZ[:, 0:2048].rearrange("p (k c) -> p k c", c=T)
        kn_b = regZ[:, 2048:4096].rearrange("p (k c) -> p k c", c=T)
        Vt = regZ[:, 4096:6144].rearrange("p (k c) -> p k c", c=T)
        PT = [regZ[:, 6144 + 512 * i:6144 + 512 * (i + 1)] for i in range(3)]
        sqb = regZ[:, 7680:8192]
        o_sqb = [cZ[15]]
        stg = regZ[:, 0:4096].bitcast(F32)
        o_stg = cZ[0:8]

        V, ACT, PE, POOL, SP = nc.vector, nc.scalar, nc.tensor, nc.gpsimd, nc.sync

        def pvcol(l, off, idx=0):
            c = l * PV_PER_L + off + idx
            return pvt[:, c:c + 1]

        o_wsrc = {}
        cv_state = {"n": 0}
        s_cvl = [None, None]

        def convert(name, src, dst, l, rows, cols):
            key = tk.newsem("cv_%s_%d" % (name, l))
            o = Obj()
            for r0 in range(0, rows, 128):
                for c0 in range(0, cols, 4096):
                    cw = min(4096, cols - c0)
                    n = cv_state["n"]
                    cv_state["n"] += 1
                    sl = n % 2
                    if s_cvl[sl] is None:
                        s_cvl[sl] = tk.newsem("cvl%d" % sl)
                    stf = regX[:, sl * 8192:(sl + 1) * 8192].bitcast(F32)[:, 0:cw]
                    o_stf = cX[sl * 16:(sl + 1) * 16]
                    stb = regY[:, sl * 4096:sl * 4096 + cw]
                    o_stb = cY[sl * 8:(sl + 1) * 8]
                    tk.emit("sp", lambda: SP.dma_start(out=stf, in_=src[l, r0:r0 + 128, c0:c0 + cw]),
                            [], o_stf, dsem=s_cvl[sl])
                    if n % 3 == 0:
                        tk.emit("act", lambda: ACT.copy(out=stb, in_=stf), o_stf, o_stb)
                    elif n % 3 == 1:
                        tk.emit("dve", lambda: V.tensor_copy(out=stb, in_=stf), o_stf, o_stb)
                    else:
                        tk.emit("pool", lambda: POOL.tensor_copy(out=stb, in_=stf), o_stf, o_stb)
                    tk.emit("sp", lambda: SP.dma_start(out=dst[l, r0:r0 + 128, c0:c0 + cw], in_=stb),
                            o_stb, [o], dsem=key)
            o_wsrc[(name, l)] = o

        s_c = tk.newsem("cst")

        def ld(dst, src, wr):
            tk.emit("sp", lambda: SP.dma_start(out=dst, in_=src), reads=[], writes=wr, dsem=s_c, serial=True)

        epsc = sb("epsc", [128, 1], F32)
        tk.emit("dve", lambda: V.memset(epsc[:], EPS), [], [o_const])
        ld(ident_f[:], ident_d, [o_const])
        ld(tmp[0][:, 0:128], jflip_d, [o_tmp[0]])
        tk.emit("dve", lambda: V.tensor_copy(out=jflip_b[:], in_=tmp[0][:, 0:128]), [o_tmp[0]], [o_const])
        tk.emit("dve", lambda: V.tensor_copy(out=ident_b[:], in_=ident_f[:]), [o_const], [o_const])
        tk.emit("dve", lambda: V.memset(ones_b[:], 1.0), [], [o_const])
        tk.emit("dve", lambda: V.memset(ones_f[:], 1.0), [], [o_const])
        tk.emit("dve", lambda: V.memset(onesrow_b[:], 0.0), [], [o_const])
        tk.emit("dve", lambda: V.memset(onesrow_b[0:1, :], 1.0), [], [o_const])
        tk.emit("dve", lambda: V.memset(esel_b[:], 0.0), [], [o_const])
        tk.emit("dve", lambda: V.memset(bsp_b[:], 0.0), [], [o_const])
        tk.emit("dve", lambda: V.memset(selT[:], 0.0), [], [o_selT])
        tk.emit("dve", lambda: V.memset(gatebuf[:], NEG), [], [o_gb])
        tk.emit("dve", lambda: V.memset(haloA[:], 0.0), [], [o_hA])
        tk.emit("dve", lambda: V.memset(haloB[:], 0.0), [], [o_hB])
        tk.emit("dve", lambda: V.memset(relb[:], 0.0), [], [o_const])
        xs = xT[:, 0:8, :].rearrange("p k c -> p (k c)")
        ld(xs[0:32, :], esel_d, o_x[0:8])
        tk.emit("dve", lambda: V.tensor_copy(out=esel_b[0:32, :], in_=xs[0:32, :]), o_x[0:8], [o_const])
        xs2 = xT[:, 8:12, :].rearrange("p k c -> p (k c)")
        ld(xs2[0:1, :], bsp_d, o_x[8:12])
        tk.emit("dve", lambda: V.tensor_copy(out=bsp_b[0:1, :], in_=xs2[0:1, :]), o_x[8:12], [o_const])
        xs3 = xT[:, 12:14, :].rearrange("p k c -> p (k c)")
        for r in range(PV_ROWS // 128):
            half = r % 2
            o_s = o_x[12 + half]
            tk.emit("sp", lambda: SP.dma_start(out=xs3[:, half * 128:(half + 1) * 128],
                                               in_=pv_d[r * 128:(r + 1) * 128, :]),
                    [], [o_s], dsem=s_c, serial=True)
            tk.tr(ps[4][:, 0:128], xs3[:, half * 128:(half + 1) * 128], ident_f[:], [o_s, o_const], [o_ps[4]])
            tk.emit("dve", lambda: V.tensor_copy(out=pvt[:, r * 128:(r + 1) * 128], in_=ps[4][:, 0:128]),
                    [o_ps[4]], [o_pv])
        ld(tmp[1][:, 0:128], tril_d, [o_tmp[1]])
        for lg in range(depth * 4):
            l, g = lg // 4, lg % 4
            tk.emit("sp", lambda: SP.dma_start(out=tmp[2][:, 0:128], in_=wsp_d[l, g]), [], [o_tmp[2]], dsem=s_c, serial=True)
            tk.emit("dve", lambda: V.tensor_tensor(out=tmp[2][:, 128:256], in0=tmp[2][:, 0:128],
                                                   in1=tmp[1][:, 0:128], op=ALU.mult),
                    [o_tmp[2], o_tmp[1]], [o_tmp[2]])
            tk.tr(ps[4][:, 0:128], tmp[2][:, 128:256], ident_f[:], [o_tmp[2], o_const], [o_ps[4]])
            tk.emit("dve", lambda: V.tensor_copy(out=wmT[:, lg, :], in_=ps[4][:, 0:128]), [o_ps[4]], [o_const])
        ld(relb[0:32, :], relb_d, [o_const])
        tk.emit("dve", lambda: V.tensor_scalar(out=relb_s[:], in0=relb[:], scalar1=1.0 / SCALE, scalar2=None,
                                               op0=ALU.mult), [o_const], [o_const])
        tk.emit("dve", lambda: V.memset(relb_s[32:33, :], NEG), [], [o_const])
        ld(tmp[3][:, 0:128], ohfar_d, [o_tmp[3]])
        tk.mm([(ps[5][:, 0:4], tmp[3][:, 0:128], relb[:], True, True)], [o_tmp[3], o_const], [o_ps[5]])
        tk.emit("dve", lambda: V.tensor_copy(out=biasfar[:], in_=ps[5][:, 0:4]), [o_ps[5]], [o_const])
        oh_sb = xT[:, 0:6, :].rearrange("p k c -> p (k c)")
        ld(oh_sb, oh_d, o_x[0:6])
        tt_sb = xT[:, 6:12, :].rearrange("p k c -> p (k c)")
        for c in range(LT // 512):
            tk.mm([(ps[5][0:4, :], relb_s[:], oh_sb[:, c * 512:(c + 1) * 512], True, True)],
                  o_x[0:6] + [o_const], [o_ps[5]])
            tk.emit("dve", lambda: V.tensor_copy(out=tt_sb[0:4, c * 512:(c + 1) * 512], in_=ps[5][0:4, :]),
                    [o_ps[5]], o_x[6:12])
        o_ttab = Obj()
        s_t = tk.newsem("ttab")
        tk.emit("sp", lambda: SP.dma_start(out=ttab_d[:, 0:LT], in_=tt_sb[0:4, :]), o_x[6:12], [o_ttab], dsem=s_t, serial=True)
        o_rtab = Obj()
        hk_f = xT[:, 0:6, :].rearrange("p k c -> p (k c)")[:, 0:RM]
        hk_b = hT[:, 0:6, :].rearrange("p k c -> p (k c)")[:, 0:RM]
        rt_b = hT[:, 6:12, :].rearrange("p k c -> p (k c)")[:, 0:RM]
        for h in range(4):
            hank = bass.AP(tensor=ttab_d.tensor, offset=h * (LT + 128), ap=[[1, 128], [1, RM]])
            tk.emit("sp", lambda: SP.dma_start(out=hk_f, in_=hank), [o_ttab], o_x[0:6], dsem=s_t, serial=True)
            tk.emit("dve", lambda: V.tensor_copy(out=hk_b, in_=hk_f), o_x[0:6], o_h[0:6])
            c0 = 0
            while c0 < RM:
                n = min(512, RM - c0)
                tk.mm([(ps[5][:, 0:n], jflip_b[:], hk_b[:, c0:c0 + n], True, True)], o_h[0:6] + [o_const], [o_ps[5]])
                tk.emit("act", lambda: ACT.copy(out=rt_b[:, c0:c0 + n], in_=ps[5][:, 0:n]), [o_ps[5]], o_h[6:12])
                c0 += n
            tk.emit("sp", lambda: SP.dma_start(out=rtab_d[h], in_=rt_b), o_h[6:12], [o_rtab], dsem=s_t, serial=True)

        for l in range(depth):
            convert("in", w_in_d, wb_in, l, D, IN_COLS)
            for bi in range(4):
                convert("o" + "abcd"[bi], w_oa_d[bi], wb_oa[bi], l, 512, D)
            convert("o", w_o_d, wb_o, l, D, D)
            convert("1", w_1_d, wb_1, l, D, DFF)
            convert("2", w_2_d, wb_2, l, DFF, D)

        def blocks_big():
            for i in range(n_tiles):
                for l in range(depth):
                    for c in range(10):
                        yield ("in", l), wb_in[l, :, c * 512:(c + 1) * 512].rearrange("(k p) c -> p k c", p=128), 512
                    for jg in range(4):
                        for br in range(4):
                            c0 = OFF_G + br * D + jg * 512
                            yield ("in", l), wb_in[l, :, c0:c0 + 512].rearrange("(k p) c -> p k c", p=128), 512
                    for jg in range(4):
                        yield ("o", l), wb_o[l, :, jg * 512:(jg + 1) * 512].rearrange("(k p) c -> p k c", p=128), 512
                    for hh in range(2):
                        for c in range(8):
                            c0 = hh * 4096 + c * 512
                            yield ("1", l), wb_1[l, :, c0:c0 + 512].rearrange("(k p) c -> p k c", p=128), 512
                        for j2 in range(8):
                            yield ("2", l), wb_2[l, hh * 4096:(hh + 1) * 4096, j2 * 256:(j2 + 1) * 256].rearrange(
                                "(k p) c -> p k c", p=128), 256

        def blocks_small():
            for i in range(n_tiles):
                for l in range(depth):
                    for jg in range(4):
                        for br in range(4):
                            yield ("o" + "abcd"[br], l), wb_oa[br][l, :, jg * 512:(jg + 1) * 512].rearrange(
                                "(k p) c -> p k c", p=128), 512

        class WStream:
            def __init__(self, gen, bufs, obs, name):
                self.gen = gen
                self.bufs = bufs
                self.obs = obs
                self.sems = [tk.newsem("w%s%d" % (name, i)) for i in range(len(bufs))]
                self.n = 0
                self.pending = []
                self._issue()

            def _issue(self):
                try:
                    key, src, cw = next(self.gen)
                except StopIteration:
                    return
                s = self.n % len(self.bufs)
                self.n += 1
                buf = self.bufs[s]
                if len(buf.shape) == 2:
                    view = buf[:, :].rearrange("p (k c) -> p k c", c=cw)
                else:
                    view = buf[:, :, :]
                tk.emit("sp", lambda: SP.dma_start(out=view, in_=src), [o_wsrc[key]], [self.obs[s]], dsem=self.sems[s])
                self.pending.append((view, self.obs[s]))

            def next(self):
                if len(self.pending) < 2:
                    self._issue()
                v = self.pending.pop(0)
                return v

        wS = WStream(blocks_big(), wbuf, o_wb, "b")
        wT = WStream(blocks_small(), wsm, o_ws, "s")

        s_x = tk.newsem("xio")
        s_kv = tk.newsem("kvw")
        s_kl = [tk.newsem("kld%d" % i) for i in range(2)]
        s_vl = [tk.newsem("vld%d" % i) for i in range(2)]
        s_rl = [tk.newsem("rld%d" % i) for i in range(2)]
        o_kc = [Obj() for _ in range(L)]
        o_vc = [Obj() for _ in range(L)]
        rcount = [0]
        kvcount = [0]

        def rmsnorm(gofs, l):
            for c in range(16):
                tk.emit("act", lambda: ACT.activation(out=yin[:, c, :], in_=xT[:, c, :], func=AF.Square),
                        [o_x[c]], o_yin(c))
            tk.mm(seq_group(ps[4][:], [(ones_b[:], yin[:, c, :]) for c in range(16)]),
                  sum([o_yin(c) for c in range(16)], []), [o_ps[4]])
            tk.emit("act", lambda: ACT.activation(out=tmp[0][:], in_=ps[4][:], func=AF.Sqrt, bias=epsc[:, 0:1],
                                                  scale=1.0 / D), [o_ps[4], o_const], [o_tmp[0]])
            tk.emit("dve", lambda: V.reciprocal(out=tmp[0][:], in_=tmp[0][:]), [o_tmp[0]], [o_tmp[0]])
            for c in range(16):
                tk.emit("dve", lambda: V.scalar_tensor_tensor(out=hT[:, c, :], in0=xT[:, c, :],
                                                              scalar=pvcol(l, gofs, c), in1=tmp[0][:],
                                                              op0=ALU.mult, op1=ALU.mult),
                        [o_x[c], o_tmp[0], o_pv], [o_h[c]])


        def zblock(consume, tokmajor=False):
            wv, ow = wS.next()
            for k in range(4):
                b = psnext()
                if not tokmajor:
                    tk.mm(seq_group(ps[b][:], [(wv[:, kc, k * 128:(k + 1) * 128], hT[:, kc, :]) for kc in range(16)]),
                          [ow] + o_h, [o_ps[b]])
                else:
                    tk.mm(seq_group(ps[b][:], [(hT[:, kc, k * 128:(k + 1) * 128], wv[:, kc, :]) for kc in range(16)]),
                          [ow] + o_h, [o_ps[b]])
                consume(k, b)

        def ln_stats(src, o_src, sq, o_sq):
            for k in range(4):
                tk.emit("act", lambda: ACT.activation(out=sq(k), in_=src(k), func=AF.Square), o_src(k), o_sq(k))
            tk.mm(seq_group(ps[4][:], [(ones_f[:], src(k)) for k in range(4)]),
                  sum([o_src(k) for k in range(4)], []), [o_ps[4]])
            tk.mm(seq_group(ps[5][:], [(ones_f[:], sq(k)) for k in range(4)]),
                  sum([o_sq(k) for k in range(4)], []), [o_ps[5]])
            tk.emit("act", lambda: ACT.activation(out=tmp[1][:], in_=ps[4][:], func=AF.Copy, scale=1.0 / 512),
                    [o_ps[4]], [o_tmp[1]])
            tk.emit("dve", lambda: V.tensor_tensor(out=tmp[3][:], in0=tmp[1][:], in1=tmp[1][:], op=ALU.mult),
                    [o_tmp[1]], [o_tmp[3]])
            tk.emit("dve", lambda: V.scalar_tensor_tensor(out=tmp[2][:], in0=ps[5][:], scalar=1.0 / 512, in1=tmp[3][:],
                                                          op0=ALU.mult, op1=ALU.subtract),
                    [o_ps[5], o_tmp[3]], [o_tmp[2]])
            tk.emit("act", lambda: ACT.activation(out=tmp[2][:], in_=tmp[2][:], func=AF.Sqrt, bias=epsc[:, 0:1],
                                                  scale=1.0), [o_tmp[2], o_const], [o_tmp[2]])
            tk.emit("dve", lambda: V.reciprocal(out=tmp[2][:], in_=tmp[2][:]), [o_tmp[2]], [o_tmp[2]])

        def layer(i, l):
            rmsnorm(PV_NM, l)
            zblock(lambda k, b: tk.emit("act", lambda: ACT.copy(out=zc[0][:, k, :], in_=ps[b][:]), [o_ps[b]], o_zc(0, k)))
            zblock(lambda k, b: tk.emit("act", lambda: ACT.copy(out=zc[1][:, k, :], in_=ps[b][:]), [o_ps[b]], o_zc(1, k)))
            zblock(lambda k, b: tk.emit("dve", lambda: V.tensor_tensor(out=bufA[:, k, 2:514], in0=ps[b][:],
                                                                      in1=zc[1][:, k, :], op=ALU.mult),
                                        [o_ps[b]] + o_zc(1, k), o_bufA))
            tk.emit("pool", lambda: POOL.tensor_copy(out=bufA[:, :, 0:2], in_=haloA[:, l, :, :]), [o_hA], o_bufA)
            for k in range(4):
                tk.emit("dve", lambda: V.tensor_scalar(out=tmp[1][:], in0=bufA[:, k, 0:512],
                                                       scalar1=pvcol(l, PV_CAW, 0 * 4 + k), scalar2=None, op0=ALU.mult),
                        o_bufA + [o_pv], [o_tmp[1]])
                for tap in (1, 2):
                    tk.emit("dve", lambda: V.scalar_tensor_tensor(out=tmp[1][:], in0=bufA[:, k, tap:tap + 512],
                                                                  scalar=pvcol(l, PV_CAW, tap * 4 + k), in1=tmp[1][:],
                                                                  op0=ALU.mult, op1=ALU.add),
                            o_bufA + [o_tmp[1]], [o_tmp[1]])
                tk.emit("dve", lambda: V.tensor_tensor(out=yin[:, k, :], in0=tmp[1][:], in1=zc[0][:, k, :], op=ALU.mult),
                        [o_tmp[1]] + o_zc(0, k), o_yin(k))
            tk.emit("pool", lambda: POOL.tensor_copy(out=haloA[:, l, :, :], in_=bufA[:, :, 512:514]), o_bufA, [o_hA])
            zblock(lambda k, b: tk.emit("act", lambda: ACT.copy(out=zc[0][:, k, :], in_=ps[b][:]), [o_ps[b]], o_zc(0, k)))

            def cons_bg(k, b):
                tk.emit("act", lambda: ACT.activation(out=tmp[1][:], in_=ps[b][:], func=AF.Sigmoid), [o_ps[b]], [o_tmp[1]])
                tk.emit("dve", lambda: V.tensor_tensor(out=bufB[:, k, 30:542], in0=zc[0][:, k, :], in1=tmp[1][:],
                                                       op=ALU.mult), o_zc(0, k) + [o_tmp[1]], o_bufB)
            zblock(cons_bg)
            tk.emit("pool", lambda: POOL.tensor_copy(out=bufB[:, :, 0:30], in_=haloB[:, l, :, :]), [o_hB], o_bufB)
            for k in range(4):
                tk.emit("dve", lambda: V.tensor_scalar(out=zc[1][:, k, :], in0=bufB[:, k, 0:512],
                                                       scalar1=pvcol(l, PV_CBW, k), scalar2=pvcol(l, PV_CBB, k),
                                                       op0=ALU.mult, op1=ALU.add),
                        o_bufB + [o_pv], o_zc(1, k))
                for tap in range(1, 31):
                    tk.emit("dve", lambda: V.scalar_tensor_tensor(out=zc[1][:, k, :], in0=bufB[:, k, tap:tap + 512],
                                                                  scalar=pvcol(l, PV_CBW, tap * 4 + k),
                                                                  in1=zc[1][:, k, :], op0=ALU.mult, op1=ALU.add),
                            o_bufB + o_zc(1, k), o_zc(1, k))
            tk.emit("pool", lambda: POOL.tensor_copy(out=haloB[:, l, :, :], in_=bufB[:, :, 512:542]), o_bufB, [o_hB])
            ln_stats(lambda k: zc[1][:, k, :], lambda k: o_zc(1, k), lambda k: zc[0][:, k, :], lambda k: o_zc(0, k))
            for k in range(4):
                tk.emit("dve", lambda: V.tensor_tensor(out=zc[1][:, k, :], in0=zc[1][:, k, :], in1=tmp[1][:],
                                                       op=ALU.subtract), o_zc(1, k) + [o_tmp[1]], o_zc(1, k))
                tk.emit("dve", lambda: V.tensor_tensor(out=zc[1][:, k, :], in0=zc[1][:, k, :], in1=tmp[2][:],
                                                       op=ALU.mult), o_zc(1, k) + [o_tmp[2]], o_zc(1, k))
                tk.emit("act", lambda: ACT.activation(out=yin[:, 4 + k, :], in_=zc[1][:, k, :], func=AF.Silu,
                                                      bias=pvcol(l, PV_LBB, k), scale=pvcol(l, PV_LBG, k)),
                        o_zc(1, k) + [o_pv], o_yin(4 + k))
            zblock(lambda k, b: tk.emit("act", lambda: ACT.activation(out=zc[0][:, k, :], in_=ps[b][:],
                                                                     func=AF.Gelu_apprx_tanh), [o_ps[b]], o_zc(0, k)))
            zblock(lambda k, b: tk.emit("act", lambda: ACT.activation(out=zc[1][:, k, :], in_=ps[b][:],
                                                                     func=AF.Gelu_apprx_tanh), [o_ps[b]], o_zc(1, k)))
            ln_stats(lambda k: zc[1][:, k, :], lambda k: o_zc(1, k), lambda k: bufA[:, k, 0:512], lambda k: o_bufA)
            for g in range(4):
                tk.emit("dve", lambda: V.tensor_tensor(out=zc[1][:, g, :], in0=zc[1][:, g, :], in1=tmp[1][:],
                                                       op=ALU.subtract), o_zc(1, g) + [o_tmp[1]], o_zc(1, g))
                tk.emit("dve", lambda: V.tensor_tensor(out=zc[1][:, g, :], in0=zc[1][:, g, :], in1=tmp[2][:],
                                                       op=ALU.mult), o_zc(1, g) + [o_tmp[2]], o_zc(1, g))
                tk.emit("act", lambda: ACT.activation(out=zc[1][:, g, :], in_=zc[1][:, g, :], func=AF.Identity,
                                                      bias=pvcol(l, PV_LCB, g), scale=pvcol(l, PV_LCG, g)),
                        o_zc(1, g) + [o_pv], o_zc(1, g))
                b = psnext()
                for n in range(4):
                    tk.tr(ps[b][:, n * 128:(n + 1) * 128], zc[1][:, g, n * 128:(n + 1) * 128], ident_f[:],
                          o_zc(1, g) + [o_const], [o_ps[b]])
                tk.emit("dve", lambda: V.tensor_copy(out=vT[:, :, :], in_=ps[b][:].rearrange("p (k c) -> p k c", c=128)),
                        [o_ps[b]], [o_vT])
                b2 = psnext()
                lg = l * 4 + g
                for n in range(4):
                    tk.mm([(ps[b2][:, n * 128:(n + 1) * 128], vT[:, n, :], wmT[:, lg, :], True, False),
                           (ps[b2][:, n * 128:(n + 1) * 128], onesrow_b[:], bsp_b[:, lg * 128:(lg + 1) * 128], False, True)],
                          [o_vT, o_const], [o_ps[b2]])
                tk.emit("dve", lambda: V.tensor_tensor(out=yin[:, 8 + g, :], in0=ps[b2][:], in1=zc[0][:, g, :], op=ALU.mult),
                        [o_ps[b2]] + o_zc(0, g), o_yin(8 + g))
            zblock(lambda k, b: tk.emit("act", lambda: ACT.copy(out=zc[0][:, k, :], in_=ps[b][:]), [o_ps[b]], o_zc(0, k)))
            zblock(lambda k, b: tk.emit("act", lambda: ACT.copy(out=zc[1][:, k, :], in_=ps[b][:]), [o_ps[b]], o_zc(1, k)))
            zblock(lambda k, b: tk.emit("act", lambda: ACT.copy(out=Vt[:, k, :], in_=ps[b][:]), [o_ps[b]], [cZ[8 + k]]),
                   tokmajor=True)
            for z, gofs, dstb, cbase in ((0, PV_QG, qn_b, 0), (1, PV_KG, kn_b, 4)):
                for h in range(4):
                    tk.emit("act", lambda: ACT.activation(out=sqb, in_=zc[z][:, h, :], func=AF.Square), o_zc(z, h), o_sqb)
                    tk.mm([(ps[4][:], ones_b[:], sqb, True, True)], o_sqb + [o_const], [o_ps[4]])
                    tk.emit("act", lambda: ACT.activation(out=tmp[0][:], in_=ps[4][:], func=AF.Sqrt, bias=epsc[:, 0:1],
                                                          scale=1.0 / 128), [o_ps[4], o_const], [o_tmp[0]])
                    tk.emit("dve", lambda: V.reciprocal(out=tmp[0][:], in_=tmp[0][:]), [o_tmp[0]], [o_tmp[0]])
                    tk.emit("dve", lambda: V.scalar_tensor_tensor(out=zc[z][:, h, :], in0=zc[z][:, h, :],
                                                                  scalar=pvcol(l, gofs), in1=tmp[0][:],
                                                                  op0=ALU.mult, op1=ALU.mult),
                            o_zc(z, h) + [o_tmp[0], o_pv], o_zc(z, h))
                    tk.emit("pool", lambda: POOL.tensor_copy(out=dstb[:, h, :], in_=zc[z][:, h, :]), o_zc(z, h), [cZ[cbase + h]])
            for h in range(4):
                tk.emit("dve", lambda: V.tensor_reduce(out=kmean[:, l, h, 2 * i:2 * i + 2],
                                                       in_=zc[1][:, h, :].rearrange("p (b t) -> p b t", t=256),
                                                       axis=AX.X, op=ALU.add), o_zc(1, h), [o_km])
            tk.emit("sp", lambda: SP.dma_start(out=kc_d[l, :, :, i * T:(i + 1) * T].rearrange("h p t -> p h t"),
                                               in_=kn_b[:, :, :]), cZ[4:8], [o_kc[l]], dsem=s_kv)
            for h in range(4):
                tk.emit("sp", lambda: SP.dma_start(out=vc_d[l, h, :, 4 * i:4 * i + 4, :],
                                                   in_=Vt[:, :, h * 128:(h + 1) * 128]),
                        cZ[8:12], [o_vc[l]], dsem=s_kv)
            nhist = 4 * i
            for h in range(4):
                anysel = False
                for qc in range(4):
                    nv = 2 * i + (1 if qc >= 2 else 0)
                    if nv == 0:
                        continue
                    anysel = True
                    tk.mm([(ps[5][:, qc * 64:qc * 64 + nv], zc[0][:, h, qc * 128:(qc + 1) * 128],
                            kmean[:, l, h, 0:nv], True, True)], o_zc(0, h) + [o_km], [o_ps[5]])
                    tk.emit("dve", lambda: V.tensor_copy(out=gatebuf[:, qc, 0:nv], in_=ps[5][:, qc * 64:qc * 64 + nv]),
                            [o_ps[5]], [o_gb])
                    tk.emit("dve", lambda: V.max(out=m8[:, qc, :], in_=gatebuf[:, qc, 0:max(nv, 8)]), [o_gb], [o_m8])
                    tk.emit("dve", lambda: V.tensor_scalar(out=thr[:, qc:qc + 1], in0=m8[:, qc, 2:3], scalar1=-1e29,
                                                           scalar2=None, op0=ALU.max), [o_m8], [o_thr])
                    tk.emit("dve", lambda: V.tensor_scalar(out=selb[:, qc, 0:32], in0=gatebuf[:, qc, 0:32],
                                                           scalar1=thr[:, qc:qc + 1], scalar2=None, op0=ALU.is_ge),
                            [o_gb, o_thr], [o_selb])
                    tk.emit("dve", lambda: V.tensor_scalar(out=selb[:, qc, 0:32], in0=selb[:, qc, 0:32],
                                                           scalar1=-1.0, scalar2=1e30, op0=ALU.add, op1=ALU.mult),
                            [o_selb], [o_selb])
                    tk.mm([(ps[4][0:32, qc * 128:(qc + 1) * 128], selb[:, qc, 0:32], ident_b[:], True, True)],
                          [o_selb, o_const], [o_ps[4]])
                    tk.emit("act", lambda: ACT.copy(out=selT[0:32, qc * 128:(qc + 1) * 128],
                                                    in_=ps[4][0:32, qc * 128:(qc + 1) * 128]), [o_ps[4]], [o_selT])
                rs = rcount[0] % 2
                rcount[0] += 1
                tk.emit("sp", lambda: SP.dma_start(out=Rb[rs][:], in_=rtab_d[h]), [o_rtab], [o_Rb[rs]], dsem=s_rl[rs])
                tiles = [("h", kt) for kt in range(nhist)] + [("o", j) for j in range(4)]
                nt = len(tiles)
                pti = 0
                kvs = None
                for ti, (kind, kt) in enumerate(tiles):
                    b = psnext()
                    if kind == "h":
                        if kt % 16 == 0:
                            kvs = kvcount[0] % 2
                            kvcount[0] += 1
                            nk = min(16, nhist - kt)
                            tk.emit("sp", lambda: SP.dma_start(out=kbuf[kvs][:, 0:nk * 128],
                                                               in_=kc_d[l, h, :, kt * 128:(kt + nk) * 128]),
                                    [o_kc[l]], o_kb(kvs), dsem=s_kl[kvs])
                            tk.emit("sp", lambda: SP.dma_start(out=vbuf[kvs][:, 0:nk, :],
                                                               in_=vc_d[l, h, :, kt:kt + nk, :]),
                                    [o_vc[l]], o_vb(kvs), dsem=s_vl[kvs])
                        kk = kt % 16
                        q0 = 0
                        c_off = i * T - kt * 128
                        n_blk = kt // 2
                        grp = [(ps[b][:], kbuf[kvs][:, kk * 128:(kk + 1) * 128], qn_b[:, h, :]),
                               (ps[b][:], esel_b[:, n_blk * 128:(n_blk + 1) * 128], selT[:, :])]
                        rds = o_kb(kvs) + [cZ[h], o_selT, o_const]
                        near = c_off <= 2048
                        if near:
                            grp.append((ps[b][:], ident_b[:], Rb[rs][:, c_off + 384:c_off + 384 + 512]))
                            rds = rds + [o_Rb[rs]]
                        v_l = vbuf[kvs][:, kk, :]
                        v_o = o_vb(kvs)
                    else:
                        j = kt
                        q0 = 128 * j
                        c_off = -128 * j
                        grp = [(ps[b][:, q0:512], kn_b[:, h, j * 128:(j + 1) * 128], qn_b[:, h, q0:512])]
                        rds = [cZ[4 + h], cZ[h], o_const, o_Rb[rs]]
                        if j < 2 and i * 2 + 1 > 0:
                            grp.append((ps[b][:, 256:512], esel_b[:, (2 * i) * 128:(2 * i + 1) * 128], selT[:, 256:512]))
                            rds = rds + [o_selT]
                        grp.append((ps[b][:, q0:512], ident_b[:], Rb[rs][:, c_off + 384 + q0:c_off + 384 + 512]))
                        near = True
                        v_l = Vt[:, j, h * 128:(h + 1) * 128]
                        v_o = [cZ[8 + j]]
                    ng = len(grp)
                    tk.mm([(o, a, r, gi == 0, gi == ng - 1) for gi, (o, a, r) in enumerate(grp)], rds, [o_ps[b]])
                    p = pti % 3
                    pti += 1
                    if near:
                        tk.emit("act", lambda: ACT.activation(out=PT[p][:, q0:512], in_=ps[b][:, q0:512], func=AF.Exp,
                                                              scale=SCALE), [o_ps[b]], [cZ[12 + p]])
                    else:
                        tk.emit("act", lambda: ACT.activation(out=PT[p][:, q0:512], in_=ps[b][:, q0:512], func=AF.Exp,
                                                              bias=biasfar[:, h:h + 1], scale=SCALE),
                                [o_ps[b], o_const], [cZ[12 + p]])
                    tk.mm([(ps[6][:, q0:512], v_l, PT[p][:, q0:512], ti == 0, ti == nt - 1),
                           (ps[7][:, q0:512], ones_b[:], PT[p][:, q0:512], ti == 0, ti == nt - 1)],
                          v_o + [cZ[12 + p], o_const], [o_ps[6], o_ps[7]])
                tk.emit("dve", lambda: V.reciprocal(out=tmp[3][:], in_=ps[7][:]), [o_ps[7]], [o_tmp[3]])
                tk.emit("dve", lambda: V.tensor_tensor(out=yin[:, 12 + h, :], in0=ps[6][:], in1=tmp[3][:], op=ALU.mult),
                        [o_ps[6], o_tmp[3]], o_yin(12 + h))
            for jg in range(4):
                for br in range(4):
                    wv, ow = wS.next()
                    wsv, ows = wT.next()
                    for k in range(4):
                        bg = psnext()
                        tk.mm(seq_group(ps[bg][:], [(wv[:, kc, k * 128:(k + 1) * 128], hT[:, kc, :]) for kc in range(16)]),
                              [ow] + o_h, [o_ps[bg]])
                        by = psnext()
                        tk.mm(seq_group(ps[by][:], [(wsv[:, kc, k * 128:(k + 1) * 128], yin[:, br * 4 + kc, :])
                                                    for kc in range(4)]),
                              [ows] + sum([o_yin(br * 4 + kc) for kc in range(4)], []), [o_ps[by]])
                        tt = 1 + (k % 2)
                        tk.emit("act", lambda: ACT.activation(out=tmp[tt][:], in_=ps[bg][:], func=AF.Sigmoid),
                                [o_ps[bg]], [o_tmp[tt]])
                        if br == 0:
                            tk.emit("dve", lambda: V.tensor_tensor(out=zc[0][:, k, :], in0=ps[by][:], in1=tmp[tt][:],
                                                                   op=ALU.mult), [o_ps[by], o_tmp[tt]], o_zc(0, k))
                        else:
                            tk.emit("dve", lambda: V.tensor_tensor(out=tmp[tt][:], in0=ps[by][:], in1=tmp[tt][:],
                                                                   op=ALU.mult), [o_ps[by], o_tmp[tt]], [o_tmp[tt]])
                            tk.emit("pool", lambda: POOL.tensor_tensor(out=zc[0][:, k, :], in0=zc[0][:, k, :],
                                                                       in1=tmp[tt][:], op=ALU.add),
                                    o_zc(0, k) + [o_tmp[tt]], o_zc(0, k))
                for k in range(4):
                    tk.emit("act", lambda: ACT.copy(out=mrg[:, jg * 4 + k, :], in_=zc[0][:, k, :]), o_zc(0, k), o_mrg(jg * 4 + k))
            for jg in range(4):
                wv, ow = wS.next()
                for k in range(4):
                    b = psnext()
                    c = jg * 4 + k
                    tk.mm(seq_group(ps[b][:], [(wv[:, kc, k * 128:(k + 1) * 128], mrg[:, kc, :]) for kc in range(16)]),
                          [ow] + sum([o_mrg(kc) for kc in range(16)], []), [o_ps[b]])
                    tk.emit("dve", lambda: V.tensor_tensor(out=xT[:, c, :], in0=ps[b][:], in1=xT[:, c, :], op=ALU.add),
                            [o_ps[b], o_x[c]], [o_x[c]])
            rmsnorm(PV_NF, l)
            for hh in range(2):
                for c8 in range(8):
                    wv, ow = wS.next()
                    for k in range(4):
                        b = psnext()
                        hc = c8 * 4 + k
                        tk.mm(seq_group(ps[b][:], [(wv[:, kc, k * 128:(k + 1) * 128], hT[:, kc, :]) for kc in range(16)]),
                              [ow] + o_h, [o_ps[b]])
                        tt = 1 + (k % 2)
                        tk.emit("act", lambda: ACT.activation(out=tmp[tt][:], in_=ps[b][:], func=AF.Relu),
                                [o_ps[b]], [o_tmp[tt]])
                        tk.emit("pool", lambda: POOL.tensor_tensor(out=hid[:, hc, :], in0=tmp[tt][:], in1=tmp[tt][:],
                                                                   op=ALU.mult), [o_tmp[tt]], o_hid(hc))
                for j2 in range(8):
                    wv, ow = wS.next()
                    for k in range(2):
                        b = psnext()
                        c = j2 * 2 + k
                        tk.mm(seq_group(ps[b][:], [(wv[:, kc, k * 128:(k + 1) * 128], hid[:, kc, :]) for kc in range(32)]),
                              [ow] + sum([o_hid(kc) for kc in range(32)], []), [o_ps[b]])
                        tk.emit("dve", lambda: V.tensor_tensor(out=xT[:, c, :], in0=ps[b][:], in1=xT[:, c, :], op=ALU.add),
                                [o_ps[b], o_x[c]], [o_x[c]])

        for i in range(n_tiles):
            for n in range(4):
                tk.emit("sp", lambda: SP.dma_start(out=stg[:, :], in_=x_d[i * T + n * 128:i * T + (n + 1) * 128, :]),
                        [], o_stg, dsem=s_x)
                for c4 in range(4):
                    b = psnext()
                    for cc in range(4):
                        c = c4 * 4 + cc
                        tk.tr(ps[b][:, cc * 128:(cc + 1) * 128], stg[:, c * 128:(c + 1) * 128], ident_f[:],
                              o_stg + [o_const], [o_ps[b]])
                    tk.emit("dve", lambda: V.tensor_copy(out=xT[:, c4 * 4:(c4 + 1) * 4, n * 128:(n + 1) * 128],
                                                         in_=ps[b][:].rearrange("p (k c) -> p k c", c=128)),
                            [o_ps[b]], o_x[c4 * 4:(c4 + 1) * 4])
            for l in range(depth):
                layer(i, l)
            for n in range(4):
                for c4 in range(4):
                    b = psnext()
                    for cc in range(4):
                        c = c4 * 4 + cc
                        tk.tr(ps[b][:, cc * 128:(cc + 1) * 128], xT[:, c, n * 128:(n + 1) * 128], ident_f[:],
                              [o_x[c], o_const], [o_ps[b]])
                    tk.emit("dve", lambda: V.tensor_copy(out=stg[:, c4 * 512:(c4 + 1) * 512], in_=ps[b][:]),
                            [o_ps[b]], o_stg)
                tk.emit("sp", lambda: SP.dma_start(out=y_d[i * T + n * 128:i * T + (n + 1) * 128, :], in_=stg[:, :]),
                        o_stg, [], dsem=s_x)
        SP.wait_ge(tk.sem[s_x], tk.cnt[s_x])
        print("instructions emitted:", tk.n_inst)
    return nc


def _rel_bucket_np(d):
    n = np.maximum(d, 0)
    nf = np.maximum(n, 1).astype(np.float32)
    large = 16 + (np.log(nf / np.float32(16)) / np.float32(math.log(2048 / 16)) * np.float32(16)).astype(np.int32)
    large = np.minimum(large, 31)
    return np.where(n < 16, n, large)


def host_consts():
    c = {}
    d = np.arange(LT) - 511
    bucket = _rel_bucket_np(d)
    bucket = np.where(d < 0, 32, bucket)
    oh = np.zeros((128, LT), np.float32)
    oh[bucket, np.arange(LT)] = 1.0
    c["c_oh"] = oh
    ohfar = np.zeros((128, 128), np.float32)
    ohfar[31, :] = 1.0
    c["c_ohfar"] = ohfar
    c["c_ident"] = np.eye(128, dtype=np.float32)
    c["c_jflip"] = np.ascontiguousarray(np.eye(128, dtype=np.float32)[::-1])
    es = np.zeros((32, 4096), np.float32)
    for n in range(32):
        es[n, n * 128:(n + 1) * 128] = 1.0
    c["c_esel"] = es
    c["c_tril"] = np.tril(np.ones((128, 128), np.float32))
    return c


def pack_pv(inp, depth=L):
    rows = []
    for l in range(L):
        rows += [inp["norm_mix_g"][l].reshape(16, 128), inp["norm_mlp_g"][l].reshape(16, 128),
                 inp["conv_a_w"][l].reshape(12, 128), inp["conv_b_w"][l].reshape(124, 128),
                 inp["conv_b_bias"][l].reshape(4, 128), inp["ln_b_g"][l].reshape(4, 128),
                 inp["ln_b_b"][l].reshape(4, 128), inp["ln_c_g"][l].reshape(4, 128),
                 inp["ln_c_b"][l].reshape(4, 128), inp["q_norm_g"][l].reshape(1, 128),
                 inp["k_norm_g"][l].reshape(1, 128)]
    pv = np.concatenate(rows, axis=0).astype(np.float32)
    out = np.zeros((PV_ROWS, 128), np.float32)
    out[:pv.shape[0]] = pv
    return out


def make_in_map(inp, b, s_len=SEQ):
    f = lambda a: np.ascontiguousarray(np.asarray(a, dtype=np.float32))
    m = {"x": f(inp["x"][b, :s_len]), "rel_bias": f(inp["rel_bias"]), "w_in": f(inp["w_in"]),
         "w_out_a": f(inp["w_out_a"]), "w_out_b": f(inp["w_out_b"]), "w_out_c": f(inp["w_out_c"]),
         "w_out_d": f(inp["w_out_d"]), "w_o": f(inp["w_o"]), "w_mlp_in": f(inp["w_mlp_in"]),
         "w_mlp_out": f(inp["w_mlp_out"]), "w_spatial": f(inp["w_spatial"]),
         "pv": pack_pv(inp), "bsp": f(inp["b_spatial"]).reshape(1, -1)}
    m.update(host_consts())
    return m


_NC_CACHE = {}


def kernel(**inputs):
    inp = {k: np.asarray(v) for k, v in inputs.items()}
    if "full" not in _NC_CACHE:
        _NC_CACHE["full"] = build(SEQ // T, L, SEQ)
    nc = _NC_CACHE["full"]
    in_maps = [make_in_map(inp, c % BATCH) for c in range(8)]
    res = run_bass_kernel_spmd(nc, in_maps, core_ids=list(range(8)))
    out = np.stack([np.asarray(res.results[b]["y"], dtype=np.float32) for b in range(BATCH)], axis=0)
    return out
```

```python
import math
import numpy as np
from contextlib import ExitStack
import concourse.bass as bass
import concourse.mybir as mybir
from concourse.bass_utils import run_bass_kernel_spmd

F32 = mybir.dt.float32
BF16 = mybir.dt.bfloat16
AF = mybir.ActivationFunctionType
ALU = mybir.AluOpType
AX = mybir.AxisListType

D = 2048
SEQ = 8192
BATCH = 4
L = 4
T = 512
OFF_B, OFF_C, OFF_D, OFF_G, IN_COLS = 1536, 2560, 3584, 5120, 13312
DFF = 8192
EPS = 1e-6
NEG = -1e30
SCALE = 128 ** -0.5
RM = 2944
LT = 3072
PV_PER_L = 190
PV_ROWS = 768
PV_NM, PV_NF, PV_CAW, PV_CBW, PV_CBB, PV_LBG, PV_LBB, PV_LCG, PV_LCB, PV_QG, PV_KG = (
    0, 16, 32, 44, 168, 172, 176, 180, 184, 188, 189)


class Obj:
    __slots__ = ("w", "r")

    def __init__(self):
        self.w = None
        self.r = {}


class Trk:
    def __init__(self, nc, es):
        self.nc = nc
        self.es = es
        self.eng = {"pe": nc.tensor, "act": nc.scalar, "dve": nc.vector, "pool": nc.gpsimd, "sp": nc.sync}
        self.sem = {}
        self.cnt = {}
        self.isdma = {}
        self.waited = {e: {} for e in self.eng}
        for e in ("pe", "act", "dve", "pool"):
            self.newsem(e, False)
        self.n_inst = 0

    def newsem(self, key, dma=True):
        self.sem[key] = self.es.enter_context(self.nc.semaphore("s_" + key))
        self.cnt[key] = 0
        self.isdma[key] = dma
        return key

    def _wait(self, e, need):
        for key, val in need.items():
            if key == "pe" and e == "pe":
                continue
            if self.isdma[key]:
                val = self.cnt[key]
            if self.waited[e].get(key, 0) >= val:
                continue
            self.eng[e].wait_ge(self.sem[key], val)
            self.waited[e][key] = val

    def emit(self, e, fn, reads=(), writes=(), dsem=None, serial=False):
        need = {}
        for o in reads:
            if o.w is not None:
                k, v = o.w
                if need.get(k, 0) < v:
                    need[k] = v
        for o in writes:
            if o.w is not None:
                k, v = o.w
                if need.get(k, 0) < v:
                    need[k] = v
            for k, v in o.r.items():
                if need.get(k, 0) < v:
                    need[k] = v
        self._wait(e, need)
        inst = fn()
        key = dsem if dsem is not None else e
        amt = 16 if dsem is not None else 1
        self.cnt[key] += amt
        inst.then_inc(self.sem[key], amt)
        ev = (key, self.cnt[key])
        for o in reads:
            if o.r.get(key, 0) < ev[1]:
                o.r[key] = ev[1]
        for o in writes:
            o.w = ev
            o.r = {}
        self.n_inst += 1
        if serial:
            self.eng[e].wait_ge(self.sem[key], self.cnt[key])
            self.waited[e][key] = self.cnt[key]
        return ev

    def mm(self, mms, reads, writes):
        need = {}
        for o in reads:
            if o.w is not None:
                k, v = o.w
                if need.get(k, 0) < v:
                    need[k] = v
        for o in writes:
            if o.w is not None:
                k, v = o.w
                if need.get(k, 0) < v:
                    need[k] = v
            for k, v in o.r.items():
                if need.get(k, 0) < v:
                    need[k] = v
        self._wait("pe", need)
        inst = None
        for (out, lhsT, rhs, st, sp) in mms:
            inst = self.nc.tensor.matmul(out, lhsT=lhsT, rhs=rhs, start=st, stop=sp)
        self.cnt["pe"] += 1
        inst.then_inc(self.sem["pe"], 1)
        ev = ("pe", self.cnt["pe"])
        for o in reads:
            o.r["pe"] = ev[1]
        for o in writes:
            o.w = ev
            o.r = {}
        self.n_inst += len(mms)
        return ev

    def tr(self, out, in_, ident, reads, writes):
        need = {}
        for o in reads:
            if o.w is not None:
                k, v = o.w
                need[k] = max(need.get(k, 0), v)
        for o in writes:
            if o.w is not None:
                k, v = o.w
                need[k] = max(need.get(k, 0), v)
            for k, v in o.r.items():
                need[k] = max(need.get(k, 0), v)
        self._wait("pe", need)
        inst = self.nc.tensor.transpose(out, in_, ident)
        self.cnt["pe"] += 1
        inst.then_inc(self.sem["pe"], 1)
        ev = ("pe", self.cnt["pe"])
        for o in reads:
            o.r["pe"] = ev[1]
        for o in writes:
            o.w = ev
            o.r = {}
        self.n_inst += 1


def seq_group(pairs_out, lst):
    n = len(lst)
    return [(pairs_out, a, b, i == 0, i == n - 1) for i, (a, b) in enumerate(lst)]


def build(n_tiles=16, depth=L, s_len=SEQ):
    nc = bass.Bass("TRN2", target_bir_lowering=False)
    NKT = s_len // 128

    def din(name, shape):
        return nc.dram_tensor(name, list(shape), F32, kind="ExternalInput").ap()

    x_d = din("x", [s_len, D])
    relb_d = din("rel_bias", [32, 4])
    w_in_d = din("w_in", [L, D, IN_COLS])
    w_oa_d = [din("w_out_" + c, [L, 512, D]) for c in "abcd"]
    w_o_d = din("w_o", [L, D, D])
    w_1_d = din("w_mlp_in", [L, D, DFF])
    w_2_d = din("w_mlp_out", [L, DFF, D])
    wsp_d = din("w_spatial", [L, 4, 128, 128])
    pv_d = din("pv", [PV_ROWS, 128])
    bsp_d = din("bsp", [1, L * 4 * 128])
    oh_d = din("c_oh", [128, LT])
    ohfar_d = din("c_ohfar", [128, 128])
    ident_d = din("c_ident", [128, 128])
    jflip_d = din("c_jflip", [128, 128])
    esel_d = din("c_esel", [32, 4096])
    tril_d = din("c_tril", [128, 128])
    y_d = nc.dram_tensor("y", [s_len, D], F32, kind="ExternalOutput").ap()

    def dint(name, shape, dt=BF16):
        return nc.dram_tensor(name, list(shape), dt, kind="Internal").ap()

    wb_in = dint("wb_in", [L, D, IN_COLS])
    wb_oa = [dint("wb_o" + c, [L, 512, D]) for c in "abcd"]
    wb_o = dint("wb_o", [L, D, D])
    wb_1 = dint("wb_1", [L, D, DFF])
    wb_2 = dint("wb_2", [L, DFF, D])
    kc_d = dint("kcache", [L, 4, 128, s_len])
    vc_d = dint("vcache", [L, 4, 128, NKT, 128])
    ttab_d = dint("ttab", [4, LT + 128], F32)
    rtab_d = dint("rtab", [4, 128, RM])

    es = ExitStack()
    with es:
        tk = Trk(nc, es)

        def sb(name, shape, dt):
            return es.enter_context(nc.sbuf_tensor(name, list(shape), dt))

        def objs(n):
            return [Obj() for _ in range(n)]

        xT = sb("xT", [128, 16, T], F32); o_x = objs(16)
        hT = sb("hT", [128, 16, T], BF16); o_h = objs(16)
        wbuf = [sb("wbuf%d" % i, [128, 8192], BF16) for i in range(2)]; o_wb = objs(2)
        wsm = [sb("wsm%d" % i, [128, 4, 512], BF16) for i in range(2)]; o_ws = objs(2)
        Rb = [sb("Rb%d" % i, [128, RM], BF16) for i in range(2)]; o_Rb = objs(2)
        regX = sb("regX", [128, 16384], BF16); cX = objs(32)
        regY = sb("regY", [128, 8192], BF16); cY = objs(16)
        regZ = sb("regZ", [128, 8704], BF16); cZ = objs(17)
        tmp = [sb("tmp%d" % i, [128, T], F32) for i in range(4)]; o_tmp = objs(4)
        pvt = sb("pvt", [128, PV_ROWS], F32); o_pv = Obj()
        ident_f = sb("ident_f", [128, 128], F32)
        ident_b = sb("ident_b", [128, 128], BF16)
        ones_b = sb("ones_b", [128, 128], BF16)
        ones_f = sb("ones_f", [128, 128], F32)
        onesrow_b = sb("onesrow_b", [128, 128], BF16)
        jflip_b = sb("jflip_b", [128, 128], BF16)
        esel_b = sb("esel_b", [128, 4096], BF16)
        bsp_b = sb("bsp_b", [128, L * 512], BF16)
        wmT = sb("wmT", [128, L * 4, 128], BF16)
        kmean = sb("kmean", [128, L, 4, 64], F32); o_km = Obj()
        haloA = sb("haloA", [128, L, 4, 2], F32); o_hA = Obj()
        haloB = sb("haloB", [128, L, 4, 30], F32); o_hB = Obj()
        selT = sb("selT", [128, T], BF16); o_selT = Obj()
        gatebuf = sb("gatebuf", [128, 4, 64], F32); o_gb = Obj()
        selb = sb("selb", [128, 4, 64], BF16); o_selb = Obj()
        m8 = sb("m8", [128, 4, 8], F32); o_m8 = Obj()
        thr = sb("thr", [128, 4], F32); o_thr = Obj()
        biasfar = sb("biasfar", [128, 4], F32)
        relb = sb("relb", [128, 4], F32)
        relb_s = sb("relb_s", [128, 4], F32)
        vT = sb("vT", [128, 4, 128], BF16); o_vT = Obj()
        o_const = Obj()

        ps = [es.enter_context(nc.psum_tensor("ps%d" % i, [128, 512], F32)) for i in range(8)]
        o_ps = objs(8)
        rot = [0]

        def psnext():
            i = rot[0]
            rot[0] = (i + 1) % 4
            return i

        zc = [regX[:, 0:4096].bitcast(F32).rearrange("p (k c) -> p k c", c=T),
              regX[:, 4096:8192].bitcast(F32).rearrange("p (k c) -> p k c", c=T)]

        def o_zc(z, k):
            return cX[z * 8 + 2 * k: z * 8 + 2 * k + 2]
        yin = regX[:, 8192:16384].rearrange("p (k c) -> p k c", c=T)

        def o_yin(c):
            return [cX[16 + c]]
        hid = regX[:, :].rearrange("p (k c) -> p k c", c=T)

        def o_hid(c):
            return [cX[c]]
        kbuf = [regY[:, 0:2048], regY[:, 2048:4096]]
        vbuf = [regY[:, 4096:6144].rearrange("p (k c) -> p k c", c=128),
                regY[:, 6144:8192].rearrange("p (k c) -> p k c", c=128)]

        def o_kb(s):
            return cY[4 * s:4 * s + 4]

        def o_vb(s):
            return cY[8 + 4 * s:8 + 4 * s + 4]
        mrg = regY[:, :].rearrange("p (k c) -> p k c", c=T)

        def o_mrg(c):
            return [cY[c]]
        bufA = regZ[:, 0:4112].bitcast(F32).rearrange("p (k c) -> p k c", c=514)
        o_bufA = cZ[0:9]
        bufB = regZ[:, 4112:6280].rearrange("p (k c) -> p k c", c=542)
        o_bufB = cZ[8:13]
        dg = sb("dg", [128, 16, 128], BF16); o_dg = objs(16)
        dgc = [0]
        qn_b = regZ[:, 0:2048].rearrange("p (k c) -> p k c", c=T)
        kn_b = regZ[:, 2048:4096].rearrange("p (k c) -> p k c", c=T)
        Vt = regZ[:, 4096:6144].rearrange("p (k c) -> p k c", c=T)
        PT = [regZ[:, 6144 + 512 * i:6144 + 512 * (i + 1)] for i in range(3)]
        sqb = regZ[:, 7680:8192]
        o_sqb = [cZ[15]]
        stg = regZ[:, 0:4096].bitcast(F32)
        o_stg = cZ[0:8]

        V, ACT, PE, POOL, SP = nc.vector, nc.scalar, nc.tensor, nc.gpsimd, nc.sync

        def pvcol(l, off, idx=0):
            c = l * PV_PER_L + off + idx
            return pvt[:, c:c + 1]

        o_wsrc = {}
        cv_state = {"n": 0}
        s_cvl = [None, None]

        def convert(name, src, dst, l, rows, cols):
            key = tk.newsem("cv_%s_%d" % (name, l))
            o = Obj()
            for r0 in range(0, rows, 128):
                for c0 in range(0, cols, 4096):
                    cw = min(4096, cols - c0)
                    n = cv_state["n"]
                    cv_state["n"] += 1
                    sl = n % 2
                    if s_cvl[sl] is None:
                        s_cvl[sl] = tk.newsem("cvl%d" % sl)
                    stf = regX[:, sl * 8192:(sl + 1) * 8192].bitcast(F32)[:, 0:cw]
                    o_stf = cX[sl * 16:(sl + 1) * 16]
                    stb = regY[:, sl * 4096:sl * 4096 + cw]
                    o_stb = cY[sl * 8:(sl + 1) * 8]
                    tk.emit("sp", lambda: SP.dma_start(out=stf, in_=src[l, r0:r0 + 128, c0:c0 + cw]),
                            [], o_stf, dsem=s_cvl[sl])
                    if n % 3 == 0:
                        tk.emit("act", lambda: ACT.copy(out=stb, in_=stf), o_stf, o_stb)
                    elif n % 3 == 1:
                        tk.emit("dve", lambda: V.tensor_copy(out=stb, in_=stf), o_stf, o_stb)
                    else:
                        tk.emit("pool", lambda: POOL.tensor_copy(out=stb, in_=stf), o_stf, o_stb)
                    tk.emit("sp", lambda: SP.dma_start(out=dst[l, r0:r0 + 128, c0:c0 + cw], in_=stb),
                            o_stb, [o], dsem=key)
            o_wsrc[(name, l)] = o

        s_c = tk.newsem("cst")

        def ld(dst, src, wr):
            tk.emit("sp", lambda: SP.dma_start(out=dst, in_=src), reads=[], writes=wr, dsem=s_c, serial=True)

        epsc = sb("epsc", [128, 1], F32)
        tk.emit("dve", lambda: V.memset(epsc[:], EPS), [], [o_const])
        ld(ident_f[:], ident_d, [o_const])
        ld(tmp[0][:, 0:128], jflip_d, [o_tmp[0]])
        tk.emit("dve", lambda: V.tensor_copy(out=jflip_b[:], in_=tmp[0][:, 0:128]), [o_tmp[0]], [o_const])
        tk.emit("dve", lambda: V.tensor_copy(out=ident_b[:], in_=ident_f[:]), [o_const], [o_const])
        tk.emit("dve", lambda: V.memset(ones_b[:], 1.0), [], [o_const])
        tk.emit("dve", lambda: V.memset(ones_f[:], 1.0), [], [o_const])
        tk.emit("dve", lambda: V.memset(onesrow_b[:], 0.0), [], [o_const])
        tk.emit("dve", lambda: V.memset(onesrow_b[0:1, :], 1.0), [], [o_const])
        tk.emit("dve", lambda: V.memset(esel_b[:], 0.0), [], [o_const])
        tk.emit("dve", lambda: V.memset(bsp_b[:], 0.0), [], [o_const])
        tk.emit("dve", lambda: V.memset(selT[:], 0.0), [], [o_selT])
        tk.emit("dve", lambda: V.memset(gatebuf[:], NEG), [], [o_gb])
        tk.emit("dve", lambda: V.memset(haloA[:], 0.0), [], [o_hA])
        tk.emit("dve", lambda: V.memset(haloB[:], 0.0), [], [o_hB])
        tk.emit("dve", lambda: V.memset(relb[:], 0.0), [], [o_const])
        xs = xT[:, 0:8, :].rearrange("p k c -> p (k c)")
        ld(xs[0:32, :], esel_d, o_x[0:8])
        tk.emit("dve", lambda: V.tensor_copy(out=esel_b[0:32, :], in_=xs[0:32, :]), o_x[0:8], [o_const])
        xs2 = xT[:, 8:12, :].rearrange("p k c -> p (k c)")
        ld(xs2[0:1, :], bsp_d, o_x[8:12])
        tk.emit("dve", lambda: V.tensor_copy(out=bsp_b[0:1, :], in_=xs2[0:1, :]), o_x[8:12], [o_const])
        xs3 = xT[:, 12:14, :].rearrange("p k c -> p (k c)")
        for r in range(PV_ROWS // 128):
            half = r % 2
            o_s = o_x[12 + half]
            tk.emit("sp", lambda: SP.dma_start(out=xs3[:, half * 128:(half + 1) * 128],
                                               in_=pv_d[r * 128:(r + 1) * 128, :]),
                    [], [o_s], dsem=s_c, serial=True)
            tk.tr(ps[4][:, 0:128], xs3[:, half * 128:(half + 1) * 128], ident_f[:], [o_s, o_const], [o_ps[4]])
            tk.emit("dve", lambda: V.tensor_copy(out=pvt[:, r * 128:(r + 1) * 128], in_=ps[4][:, 0:128]),
                    [o_ps[4]], [o_pv])
        ld(tmp[1][:, 0:128], tril_d, [o_tmp[1]])
        for lg in range(depth * 4):
            l, g = lg // 4, lg % 4
            tk.emit("sp", lambda: SP.dma_start(out=tmp[2][:, 0:128], in_=wsp_d[l, g]), [], [o_tmp[2]], dsem=s_c, serial=True)
            tk.emit("dve", lambda: V.tensor_tensor(out=tmp[2][:, 128:256], in0=tmp[2][:, 0:128],
                                                   in1=tmp[1][:, 0:128], op=ALU.mult),
                    [o_tmp[2], o_tmp[1]], [o_tmp[2]])
            tk.tr(ps[4][:, 0:128], tmp[2][:, 128:256], ident_f[:], [o_tmp[2], o_const], [o_ps[4]])
            tk.emit("dve", lambda: V.tensor_copy(out=wmT[:, lg, :], in_=ps[4][:, 0:128]), [o_ps[4]], [o_const])
        ld(relb[0:32, :], relb_d, [o_const])
        tk.emit("dve", lambda: V.tensor_scalar(out=relb_s[:], in0=relb[:], scalar1=1.0 / SCALE, scalar2=None,
                                               op0=ALU.mult), [o_const], [o_const])
        tk.emit("dve", lambda: V.memset(relb_s[32:33, :], NEG), [], [o_const])
        ld(tmp[3][:, 0:128], ohfar_d, [o_tmp[3]])
        tk.mm([(ps[5][:, 0:4], tmp[3][:, 0:128], relb[:], True, True)], [o_tmp[3], o_const], [o_ps[5]])
        tk.emit("dve", lambda: V.tensor_copy(out=biasfar[:], in_=ps[5][:, 0:4]), [o_ps[5]], [o_const])
        oh_sb = xT[:, 0:6, :].rearrange("p k c -> p (k c)")
        ld(oh_sb, oh_d, o_x[0:6])
        tt_sb = xT[:, 6:12, :].rearrange("p k c -> p (k c)")
        for c in range(LT // 512):
            tk.mm([(ps[5][0:4, :], relb_s[:], oh_sb[:, c * 512:(c + 1) * 512], True, True)],
                  o_x[0:6] + [o_const], [o_ps[5]])
            tk.emit("dve", lambda: V.tensor_copy(out=tt_sb[0:4, c * 512:(c + 1) * 512], in_=ps[5][0:4, :]),
                    [o_ps[5]], o_x[6:12])
        o_ttab = Obj()
        s_t = tk.newsem("ttab")
        tk.emit("sp", lambda: SP.dma_start(out=ttab_d[:, 0:LT], in_=tt_sb[0:4, :]), o_x[6:12], [o_ttab], dsem=s_t, serial=True)
        o_rtab = Obj()
        hk_f = xT[:, 0:6, :].rearrange("p k c -> p (k c)")[:, 0:RM]
        hk_b = hT[:, 0:6, :].rearrange("p k c -> p (k c)")[:, 0:RM]
        rt_b = hT[:, 6:12, :].rearrange("p k c -> p (k c)")[:, 0:RM]
        for h in range(4):
            hank = bass.AP(tensor=ttab_d.tensor, offset=h * (LT + 128), ap=[[1, 128], [1, RM]])
            tk.emit("sp", lambda: SP.dma_start(out=hk_f, in_=hank), [o_ttab], o_x[0:6], dsem=s_t, serial=True)
            tk.emit("dve", lambda: V.tensor_copy(out=hk_b, in_=hk_f), o_x[0:6], o_h[0:6])
            c0 = 0
            while c0 < RM:
                n = min(512, RM - c0)
                tk.mm([(ps[5][:, 0:n], jflip_b[:], hk_b[:, c0:c0 + n], True, True)], o_h[0:6] + [o_const], [o_ps[5]])
                tk.emit("act", lambda: ACT.copy(out=rt_b[:, c0:c0 + n], in_=ps[5][:, 0:n]), [o_ps[5]], o_h[6:12])
                c0 += n
            tk.emit("sp", lambda: SP.dma_start(out=rtab_d[h], in_=rt_b), o_h[6:12], [o_rtab], dsem=s_t, serial=True)

        for l in range(depth):
            convert("in", w_in_d, wb_in, l, D, IN_COLS)
            for bi in range(4):
                convert("o" + "abcd"[bi], w_oa_d[bi], wb_oa[bi], l, 512, D)
            convert("o", w_o_d, wb_o, l, D, D)
            convert("1", w_1_d, wb_1, l, D, DFF)
            convert("2", w_2_d, wb_2, l, DFF, D)

        def blocks_big():
            for i in range(n_tiles):
                for l in range(depth):
                    for c in range(10):
                        yield ("in", l), wb_in[l, :, c * 512:(c + 1) * 512].rearrange("(k p) c -> p k c", p=128), 512
                    for jg in range(4):
                        for br in range(4):
                            c0 = OFF_G + br * D + jg * 512
                            yield ("in", l), wb_in[l, :, c0:c0 + 512].rearrange("(k p) c -> p k c", p=128), 512
                    for jg in range(4):
                        yield ("o", l), wb_o[l, :, jg * 512:(jg + 1) * 512].rearrange("(k p) c -> p k c", p=128), 512
                    for hh in range(2):
                        for c in range(8):
                            c0 = hh * 4096 + c * 512
                            yield ("1", l), wb_1[l, :, c0:c0 + 512].rearrange("(k p) c -> p k c", p=128), 512
                        for j2 in range(8):
                            yield ("2", l), wb_2[l, hh * 4096:(hh + 1) * 4096, j2 * 256:(j2 + 1) * 256].rearrange(
                                "(k p) c -> p k c", p=128), 256

        def blocks_small():
            for i in range(n_tiles):
                for l in range(depth):
                    for jg in range(4):
                        for br in range(4):
                            yield ("o" + "abcd"[br], l), wb_oa[br][l, :, jg * 512:(jg + 1) * 512].rearrange(
                                "(k p) c -> p k c", p=128), 512

        class WStream:
            def __init__(self, gen, bufs, obs, name):
                self.gen = gen
                self.bufs = bufs
                self.obs = obs
                self.sems = [tk.newsem("w%s%d" % (name, i)) for i in range(len(bufs))]
                self.n = 0
                self.pending = []
                self._issue()

            def _issue(self):
                try:
                    key, src, cw = next(self.gen)
                except StopIteration:
                    return
                s = self.n % len(self.bufs)
                self.n += 1
                buf = self.bufs[s]
                if len(buf.shape) == 2:
                    view = buf[:, :].rearrange("p (k c) -> p k c", c=cw)
                else:
                    view = buf[:, :, :]
                tk.emit("sp", lambda: SP.dma_start(out=view, in_=src), [o_wsrc[key]], [self.obs[s]], dsem=self.sems[s])
                self.pending.append((view, self.obs[s]))

            def next(self):
                if len(self.pending) < 2:
                    self._issue()
                v = self.pending.pop(0)
                return v

        wS = WStream(blocks_big(), wbuf, o_wb, "b")
        wT = WStream(blocks_small(), wsm, o_ws, "s")

        s_x = tk.newsem("xio")
        s_kv = tk.newsem("kvw")
        s_kl = [tk.newsem("kld%d" % i) for i in range(2)]
        s_vl = [tk.newsem("vld%d" % i) for i in range(2)]
        s_rl = [tk.newsem("rld%d" % i) for i in range(2)]
        o_kc = [Obj() for _ in range(L)]
        o_vc = [Obj() for _ in range(L)]
        rcount = [0]
        kvcount = [0]

        def rmsnorm(gofs, l):
            for c in range(16):
                tk.emit("act", lambda: ACT.activation(out=yin[:, c, :], in_=xT[:, c, :], func=AF.Square),
                        [o_x[c]], o_yin(c))
            tk.mm(seq_group(ps[4][:], [(ones_b[:], yin[:, c, :]) for c in range(16)]),
                  sum([o_yin(c) for c in range(16)], []), [o_ps[4]])
            tk.emit("act", lambda: ACT.activation(out=tmp[0][:], in_=ps[4][:], func=AF.Sqrt, bias=epsc[:, 0:1],
                                                  scale=1.0 / D), [o_ps[4], o_const], [o_tmp[0]])
            tk.emit("dve", lambda: V.reciprocal(out=tmp[0][:], in_=tmp[0][:]), [o_tmp[0]], [o_tmp[0]])
            for c in range(16):
                tk.emit("dve", lambda: V.scalar_tensor_tensor(out=hT[:, c, :], in0=xT[:, c, :],
                                                              scalar=pvcol(l, gofs, c), in1=tmp[0][:],
                                                              op0=ALU.mult, op1=ALU.mult),
                        [o_x[c], o_tmp[0], o_pv], [o_h[c]])


        def zblock(consume, tokmajor=False):
            wv, ow = wS.next()
            for k in range(4):
                b = psnext()
                if not tokmajor:
                    tk.mm(seq_group(ps[b][:], [(wv[:, kc, k * 128:(k + 1) * 128], hT[:, kc, :]) for kc in range(16)]),
                          [ow] + o_h, [o_ps[b]])
                else:
                    tk.mm(seq_group(ps[b][:], [(hT[:, kc, k * 128:(k + 1) * 128], wv[:, kc, :]) for kc in range(16)]),
                          [ow] + o_h, [o_ps[b]])
                consume(k, b)

        def ln_stats(src, o_src, sq, o_sq):
            for k in range(4):
                tk.emit("act", lambda: ACT.activation(out=sq(k), in_=src(k), func=AF.Square), o_src(k), o_sq(k))
            tk.mm(seq_group(ps[4][:], [(ones_f[:], src(k)) for k in range(4)]),
                  sum([o_src(k) for k in range(4)], []), [o_ps[4]])
            tk.mm(seq_group(ps[5][:], [(ones_f[:], sq(k)) for k in range(4)]),
                  sum([o_sq(k) for k in range(4)], []), [o_ps[5]])
            tk.emit("act", lambda: ACT.activation(out=tmp[1][:], in_=ps[4][:], func=AF.Copy, scale=1.0 / 512),
                    [o_ps[4]], [o_tmp[1]])
            tk.emit("dve", lambda: V.tensor_tensor(out=tmp[3][:], in0=tmp[1][:], in1=tmp[1][:], op=ALU.mult),
                    [o_tmp[1]], [o_tmp[3]])
            tk.emit("dve", lambda: V.scalar_tensor_tensor(out=tmp[2][:], in0=ps[5][:], scalar=1.0 / 512, in1=tmp[3][:],
                                                          op0=ALU.mult, op1=ALU.subtract),
                    [o_ps[5], o_tmp[3]], [o_tmp[2]])
            tk.emit("act", lambda: ACT.activation(out=tmp[2][:], in_=tmp[2][:], func=AF.Sqrt, bias=epsc[:, 0:1],
                                                  scale=1.0), [o_tmp[2], o_const], [o_tmp[2]])
            tk.emit("dve", lambda: V.reciprocal(out=tmp[2][:], in_=tmp[2][:]), [o_tmp[2]], [o_tmp[2]])

        def layer(i, l):
            rmsnorm(PV_NM, l)
            zblock(lambda k, b: tk.emit("act", lambda: ACT.copy(out=zc[0][:, k, :], in_=ps[b][:]), [o_ps[b]], o_zc(0, k)))
            zblock(lambda k, b: tk.emit("act", lambda: ACT.copy(out=zc[1][:, k, :], in_=ps[b][:]), [o_ps[b]], o_zc(1, k)))
            zblock(lambda k, b: tk.emit("dve", lambda: V.tensor_tensor(out=bufA[:, k, 2:514], in0=ps[b][:],
                                                                      in1=zc[1][:, k, :], op=ALU.mult),
                                        [o_ps[b]] + o_zc(1, k), o_bufA))
            tk.emit("pool", lambda: POOL.tensor_copy(out=bufA[:, :, 0:2], in_=haloA[:, l, :, :]), [o_hA], o_bufA)
            for k in range(4):
                tk.emit("dve", lambda: V.tensor_scalar(out=tmp[1][:], in0=bufA[:, k, 0:512],
                                                       scalar1=pvcol(l, PV_CAW, 0 * 4 + k), scalar2=None, op0=ALU.mult),
                        o_bufA + [o_pv], [o_tmp[1]])
                for tap in (1, 2):
                    tk.emit("dve", lambda: V.scalar_tensor_tensor(out=tmp[1][:], in0=bufA[:, k, tap:tap + 512],
                                                                  scalar=pvcol(l, PV_CAW, tap * 4 + k), in1=tmp[1][:],
                                                                  op0=ALU.mult, op1=ALU.add),
                            o_bufA + [o_tmp[1]], [o_tmp[1]])
                tk.emit("dve", lambda: V.tensor_tensor(out=yin[:, k, :], in0=tmp[1][:], in1=zc[0][:, k, :], op=ALU.mult),
                        [o_tmp[1]] + o_zc(0, k), o_yin(k))
            tk.emit("pool", lambda: POOL.tensor_copy(out=haloA[:, l, :, :], in_=bufA[:, :, 512:514]), o_bufA, [o_hA])
            zblock(lambda k, b: tk.emit("act", lambda: ACT.copy(out=zc[0][:, k, :], in_=ps[b][:]), [o_ps[b]], o_zc(0, k)))

            def cons_bg(k, b):
                tk.emit("act", lambda: ACT.activation(out=tmp[1][:], in_=ps[b][:], func=AF.Sigmoid), [o_ps[b]], [o_tmp[1]])
                tk.emit("dve", lambda: V.tensor_tensor(out=bufB[:, k, 30:542], in0=zc[0][:, k, :], in1=tmp[1][:],
                                                       op=ALU.mult), o_zc(0, k) + [o_tmp[1]], o_bufB)
            zblock(cons_bg)
            tk.emit("pool", lambda: POOL.tensor_copy(out=bufB[:, :, 0:30], in_=haloB[:, l, :, :]), [o_hB], o_bufB)
            for k in range(4):
                b = psnext()
                for g8 in range(4):
                    taps = list(range(g8 * 8, min(31, g8 * 8 + 8)))
                    slot0 = (dgc[0] % 2) * 8
                    dgc[0] += 1
                    for ti_, tap in enumerate(taps):
                        tk.emit("dve", lambda: V.tensor_scalar(out=dg[:, slot0 + ti_, :], in0=ident_b[:],
                                                               scalar1=pvcol(l, PV_CBW, tap * 4 + k), scalar2=None,
                                                               op0=ALU.mult), [o_const, o_pv], [o_dg[slot0 + ti_]])
                    tk.mm([(ps[b][:], dg[:, slot0 + ti_, :], bufB[:, k, tap:tap + 512], tap == 0, tap == 30)
                           for ti_, tap in enumerate(taps)],
                          o_bufB + [o_dg[slot0 + ti_] for ti_ in range(len(taps))], [o_ps[b]])
                tk.emit("dve", lambda: V.tensor_scalar(out=zc[1][:, k, :], in0=ps[b][:], scalar1=pvcol(l, PV_CBB, k),
                                                       scalar2=None, op0=ALU.add), [o_ps[b], o_pv], o_zc(1, k))
            tk.emit("pool", lambda: POOL.tensor_copy(out=haloB[:, l, :, :], in_=bufB[:, :, 512:542]), o_bufB, [o_hB])
            ln_stats(lambda k: zc[1][:, k, :], lambda k: o_zc(1, k), lambda k: zc[0][:, k, :], lambda k: o_zc(0, k))
            for k in range(4):
                tk.emit("dve", lambda: V.tensor_tensor(out=zc[1][:, k, :], in0=zc[1][:, k, :], in1=tmp[1][:],
                                                       op=ALU.subtract), o_zc(1, k) + [o_tmp[1]], o_zc(1, k))
                tk.emit("dve", lambda: V.tensor_tensor(out=zc[1][:, k, :], in0=zc[1][:, k, :], in1=tmp[2][:],
                                                       op=ALU.mult), o_zc(1, k) + [o_tmp[2]], o_zc(1, k))
                tk.emit("act", lambda: ACT.activation(out=yin[:, 4 + k, :], in_=zc[1][:, k, :], func=AF.Silu,
                                                      bias=pvcol(l, PV_LBB, k), scale=pvcol(l, PV_LBG, k)),
                        o_zc(1, k) + [o_pv], o_yin(4 + k))
            zblock(lambda k, b: tk.emit("act", lambda: ACT.activation(out=zc[0][:, k, :], in_=ps[b][:],
                                                                     func=AF.Gelu_apprx_tanh), [o_ps[b]], o_zc(0, k)))
            zblock(lambda k, b: tk.emit("act", lambda: ACT.activation(out=zc[1][:, k, :], in_=ps[b][:],
                                                                     func=AF.Gelu_apprx_tanh), [o_ps[b]], o_zc(1, k)))
            ln_stats(lambda k: zc[1][:, k, :], lambda k: o_zc(1, k), lambda k: bufA[:, k, 0:512], lambda k: o_bufA)
            for g in range(4):
                tk.emit("dve", lambda: V.tensor_tensor(out=zc[1][:, g, :], in0=zc[1][:, g, :], in1=tmp[1][:],
                                                       op=ALU.subtract), o_zc(1, g) + [o_tmp[1]], o_zc(1, g))
                tk.emit("dve", lambda: V.tensor_tensor(out=zc[1][:, g, :], in0=zc[1][:, g, :], in1=tmp[2][:],
                                                       op=ALU.mult), o_zc(1, g) + [o_tmp[2]], o_zc(1, g))
                tk.emit("act", lambda: ACT.activation(out=zc[1][:, g, :], in_=zc[1][:, g, :], func=AF.Identity,
                                                      bias=pvcol(l, PV_LCB, g), scale=pvcol(l, PV_LCG, g)),
                        o_zc(1, g) + [o_pv], o_zc(1, g))
                b = psnext()
                for n in range(4):
                    tk.tr(ps[b][:, n * 128:(n + 1) * 128], zc[1][:, g, n * 128:(n + 1) * 128], ident_f[:],
                          o_zc(1, g) + [o_const], [o_ps[b]])
                tk.emit("dve", lambda: V.tensor_copy(out=vT[:, :, :], in_=ps[b][:].rearrange("p (k c) -> p k c", c=128)),
                        [o_ps[b]], [o_vT])
                b2 = psnext()
                lg = l * 4 + g
                for n in range(4):
                    tk.mm([(ps[b2][:, n * 128:(n + 1) * 128], vT[:, n, :], wmT[:, lg, :], True, False),
                           (ps[b2][:, n * 128:(n + 1) * 128], onesrow_b[:], bsp_b[:, lg * 128:(lg + 1) * 128], False, True)],
                          [o_vT, o_const], [o_ps[b2]])
                tk.emit("dve", lambda: V.tensor_tensor(out=yin[:, 8 + g, :], in0=ps[b2][:], in1=zc[0][:, g, :], op=ALU.mult),
                        [o_ps[b2]] + o_zc(0, g), o_yin(8 + g))
            zblock(lambda k, b: tk.emit("act", lambda: ACT.copy(out=zc[0][:, k, :], in_=ps[b][:]), [o_ps[b]], o_zc(0, k)))
            zblock(lambda k, b: tk.emit("act", lambda: ACT.copy(out=zc[1][:, k, :], in_=ps[b][:]), [o_ps[b]], o_zc(1, k)))
            zblock(lambda k, b: tk.emit("act", lambda: ACT.copy(out=Vt[:, k, :], in_=ps[b][:]), [o_ps[b]], [cZ[8 + k]]),
                   tokmajor=True)
            for z, gofs, dstb, cbase in ((0, PV_QG, qn_b, 0), (1, PV_KG, kn_b, 4)):
                for h in range(4):
                    tk.emit("act", lambda: ACT.activation(out=sqb, in_=zc[z][:, h, :], func=AF.Square), o_zc(z, h), o_sqb)
                    tk.mm([(ps[4][:], ones_b[:], sqb, True, True)], o_sqb + [o_const], [o_ps[4]])
                    tk.emit("act", lambda: ACT.activation(out=tmp[0][:], in_=ps[4][:], func=AF.Sqrt, bias=epsc[:, 0:1],
                                                          scale=1.0 / 128), [o_ps[4], o_const], [o_tmp[0]])
                    tk.emit("dve", lambda: V.reciprocal(out=tmp[0][:], in_=tmp[0][:]), [o_tmp[0]], [o_tmp[0]])
                    tk.emit("dve", lambda: V.scalar_tensor_tensor(out=zc[z][:, h, :], in0=zc[z][:, h, :],
                                                                  scalar=pvcol(l, gofs), in1=tmp[0][:],
                                                                  op0=ALU.mult, op1=ALU.mult),
                            o_zc(z, h) + [o_tmp[0], o_pv], o_zc(z, h))
                    tk.emit("pool", lambda: POOL.tensor_copy(out=dstb[:, h, :], in_=zc[z][:, h, :]), o_zc(z, h), [cZ[cbase + h]])
            for h in range(4):
                tk.emit("dve", lambda: V.tensor_reduce(out=kmean[:, l, h, 2 * i:2 * i + 2],
                                                       in_=zc[1][:, h, :].rearrange("p (b t) -> p b t", t=256),
                                                       axis=AX.X, op=ALU.add), o_zc(1, h), [o_km])
            tk.emit("sp", lambda: SP.dma_start(out=kc_d[l, :, :, i * T:(i + 1) * T].rearrange("h p t -> p h t"),
                                               in_=kn_b[:, :, :]), cZ[4:8], [o_kc[l]], dsem=s_kv)
            for h in range(4):
                tk.emit("sp", lambda: SP.dma_start(out=vc_d[l, h, :, 4 * i:4 * i + 4, :],
                                                   in_=Vt[:, :, h * 128:(h + 1) * 128]),
                        cZ[8:12], [o_vc[l]], dsem=s_kv)
            nhist = 4 * i
            for h in range(4):
                rs = rcount[0] % 2
                rcount[0] += 1
                tk.emit("sp", lambda: SP.dma_start(out=Rb[rs][:], in_=rtab_d[h]), [o_rtab], [o_Rb[rs]], dsem=s_rl[rs])
                anysel = False
                for qc in range(4):
                    nv = 2 * i + (1 if qc >= 2 else 0)
                    if nv == 0:
                        continue
                    anysel = True
                    tk.mm([(ps[5][:, qc * 64:qc * 64 + nv], zc[0][:, h, qc * 128:(qc + 1) * 128],
                            kmean[:, l, h, 0:nv], True, True)], o_zc(0, h) + [o_km], [o_ps[5]])
                    tk.emit("dve", lambda: V.tensor_copy(out=gatebuf[:, qc, 0:nv], in_=ps[5][:, qc * 64:qc * 64 + nv]),
                            [o_ps[5]], [o_gb])
                    tk.emit("dve", lambda: V.max(out=m8[:, qc, :], in_=gatebuf[:, qc, 0:max(nv, 8)]), [o_gb], [o_m8])
                    tk.emit("dve", lambda: V.tensor_scalar(out=thr[:, qc:qc + 1], in0=m8[:, qc, 2:3], scalar1=-1e29,
                                                           scalar2=None, op0=ALU.max), [o_m8], [o_thr])
                    tk.emit("dve", lambda: V.tensor_scalar(out=selb[:, qc, 0:32], in0=gatebuf[:, qc, 0:32],
                                                           scalar1=thr[:, qc:qc + 1], scalar2=None, op0=ALU.is_ge),
                            [o_gb, o_thr], [o_selb])
                    tk.emit("dve", lambda: V.tensor_scalar(out=selb[:, qc, 0:32], in0=selb[:, qc, 0:32],
                                                           scalar1=-1.0, scalar2=1e30, op0=ALU.add, op1=ALU.mult),
                            [o_selb], [o_selb])
                    tk.mm([(ps[4][0:32, qc * 128:(qc + 1) * 128], selb[:, qc, 0:32], ident_b[:], True, True)],
                          [o_selb, o_const], [o_ps[4]])
                    tk.emit("act", lambda: ACT.copy(out=selT[0:32, qc * 128:(qc + 1) * 128],
                                                    in_=ps[4][0:32, qc * 128:(qc + 1) * 128]), [o_ps[4]], [o_selT])
                tiles = [("h", kt) for kt in range(nhist)] + [("o", j) for j in range(4)]
                nt = len(tiles)
                nchunk = (nhist + 15) // 16
                base = kvcount[0]
                kvcount[0] += nchunk

                def load_chunk(c):
                    kvs_ = (base + c) % 2
                    kt0 = c * 16
                    nk = min(16, nhist - kt0)
                    tk.emit("sp", lambda: SP.dma_start(out=kbuf[kvs_][:, 0:nk * 128],
                                                       in_=kc_d[l, h, :, kt0 * 128:(kt0 + nk) * 128]),
                            [o_kc[l]], o_kb(kvs_), dsem=s_kl[kvs_])
                    tk.emit("sp", lambda: SP.dma_start(out=vbuf[kvs_][:, 0:nk, :],
                                                       in_=vc_d[l, h, :, kt0:kt0 + nk, :]),
                            [o_vc[l]], o_vb(kvs_), dsem=s_vl[kvs_])
                for c in range(min(2, nchunk)):
                    load_chunk(c)
                LA = 2
                pend = []
                for it in range(nt + LA):
                    if it < nt:
                        kind, kt = tiles[it]
                        b = psnext()
                        if kind == "h":
                            kvs = (base + kt // 16) % 2
                            kk = kt % 16
                            q0 = 0
                            c_off = i * T - kt * 128
                            n_blk = kt // 2
                            grp = [(ps[b][:], kbuf[kvs][:, kk * 128:(kk + 1) * 128], qn_b[:, h, :]),
                                   (ps[b][:], esel_b[:, n_blk * 128:(n_blk + 1) * 128], selT[:, :])]
                            rds = o_kb(kvs) + [cZ[h], o_selT, o_const]
                            near = c_off <= 2048
                            if near:
                                grp.append((ps[b][:], ident_b[:], Rb[rs][:, c_off + 384:c_off + 384 + 512]))
                                rds = rds + [o_Rb[rs]]
                            v_l = vbuf[kvs][:, kk, :]
                            v_o = o_vb(kvs)
                        else:
                            j = kt
                            q0 = 128 * j
                            c_off = -128 * j
                            grp = [(ps[b][:, q0:512], kn_b[:, h, j * 128:(j + 1) * 128], qn_b[:, h, q0:512])]
                            rds = [cZ[4 + h], cZ[h], o_const, o_Rb[rs]]
                            if j < 2:
                                grp.append((ps[b][:, 256:512], esel_b[:, (2 * i) * 128:(2 * i + 1) * 128], selT[:, 256:512]))
                                rds = rds + [o_selT]
                            grp.append((ps[b][:, q0:512], ident_b[:], Rb[rs][:, c_off + 384 + q0:c_off + 384 + 512]))
                            near = True
                            v_l = Vt[:, j, h * 128:(h + 1) * 128]
                            v_o = [cZ[8 + j]]
                        ng = len(grp)
                        tk.mm([(o, a, r, gi == 0, gi == ng - 1) for gi, (o, a, r) in enumerate(grp)], rds, [o_ps[b]])
                        p = it % 3
                        if near:
                            tk.emit("act", lambda: ACT.activation(out=PT[p][:, q0:512], in_=ps[b][:, q0:512], func=AF.Exp,
                                                                  scale=SCALE), [o_ps[b]], [cZ[12 + p]])
                        else:
                            tk.emit("act", lambda: ACT.activation(out=PT[p][:, q0:512], in_=ps[b][:, q0:512], func=AF.Exp,
                                                                  bias=biasfar[:, h:h + 1], scale=SCALE),
                                    [o_ps[b], o_const], [cZ[12 + p]])
                        pend.append((q0, v_l, v_o, p, it))
                    if it - LA >= 0:
                        (q0, v_l, v_o, p, ti) = pend.pop(0)
                        tk.mm([(ps[6][:, q0:512], v_l, PT[p][:, q0:512], ti == 0, ti == nt - 1),
                               (ps[7][:, q0:512], ones_b[:], PT[p][:, q0:512], ti == 0, ti == nt - 1)],
                              v_o + [cZ[12 + p], o_const], [o_ps[6], o_ps[7]])
                        kind2, kt2 = tiles[ti]
                        if kind2 == "h" and kt2 % 16 == 15 and kt2 // 16 + 2 < nchunk:
                            load_chunk(kt2 // 16 + 2)
                tk.emit("dve", lambda: V.reciprocal(out=tmp[3][:], in_=ps[7][:]), [o_ps[7]], [o_tmp[3]])
                tk.emit("dve", lambda: V.tensor_tensor(out=yin[:, 12 + h, :], in0=ps[6][:], in1=tmp[3][:], op=ALU.mult),
                        [o_ps[6], o_tmp[3]], o_yin(12 + h))
            for jg in range(4):
                for br in range(4):
                    wv, ow = wS.next()
                    wsv, ows = wT.next()
                    for k in range(4):
                        bg = psnext()
                        tk.mm(seq_group(ps[bg][:], [(wv[:, kc, k * 128:(k + 1) * 128], hT[:, kc, :]) for kc in range(16)]),
                              [ow] + o_h, [o_ps[bg]])
                        by = psnext()
                        tk.mm(seq_group(ps[by][:], [(wsv[:, kc, k * 128:(k + 1) * 128], yin[:, br * 4 + kc, :])
                                                    for kc in range(4)]),
                              [ows] + sum([o_yin(br * 4 + kc) for kc in range(4)], []), [o_ps[by]])
                        tt = 1 + (k % 2)
                        tk.emit("act", lambda: ACT.activation(out=tmp[tt][:], in_=ps[bg][:], func=AF.Sigmoid),
                                [o_ps[bg]], [o_tmp[tt]])
                        if br == 0:
                            tk.emit("dve", lambda: V.tensor_tensor(out=zc[0][:, k, :], in0=ps[by][:], in1=tmp[tt][:],
                                                                   op=ALU.mult), [o_ps[by], o_tmp[tt]], o_zc(0, k))
                        else:
                            tk.emit("dve", lambda: V.tensor_tensor(out=tmp[tt][:], in0=ps[by][:], in1=tmp[tt][:],
                                                                   op=ALU.mult), [o_ps[by], o_tmp[tt]], [o_tmp[tt]])
                            tk.emit("pool", lambda: POOL.tensor_tensor(out=zc[0][:, k, :], in0=zc[0][:, k, :],
                                                                       in1=tmp[tt][:], op=ALU.add),
                                    o_zc(0, k) + [o_tmp[tt]], o_zc(0, k))
                for k in range(4):
                    tk.emit("act", lambda: ACT.copy(out=mrg[:, jg * 4 + k, :], in_=zc[0][:, k, :]), o_zc(0, k), o_mrg(jg * 4 + k))
            for jg in range(4):
                wv, ow = wS.next()
                for k in range(4):
                    b = psnext()
                    c = jg * 4 + k
                    tk.mm(seq_group(ps[b][:], [(wv[:, kc, k * 128:(k + 1) * 128], mrg[:, kc, :]) for kc in range(16)]),
                          [ow] + sum([o_mrg(kc) for kc in range(16)], []), [o_ps[b]])
                    tk.emit("dve", lambda: V.tensor_tensor(out=xT[:, c, :], in0=ps[b][:], in1=xT[:, c, :], op=ALU.add),
                            [o_ps[b], o_x[c]], [o_x[c]])
            rmsnorm(PV_NF, l)
            for hh in range(2):
                for c8 in range(8):
                    wv, ow = wS.next()
                    for k in range(4):
                        b = psnext()
                        hc = c8 * 4 + k
                        tk.mm(seq_group(ps[b][:], [(wv[:, kc, k * 128:(k + 1) * 128], hT[:, kc, :]) for kc in range(16)]),
                              [ow] + o_h, [o_ps[b]])
                        tt = 1 + (k % 2)
                        tk.emit("act", lambda: ACT.activation(out=tmp[tt][:], in_=ps[b][:], func=AF.Relu),
                                [o_ps[b]], [o_tmp[tt]])
                        tk.emit("pool", lambda: POOL.tensor_tensor(out=hid[:, hc, :], in0=tmp[tt][:], in1=tmp[tt][:],
                                                                   op=ALU.mult), [o_tmp[tt]], o_hid(hc))
                for j2 in range(8):
                    wv, ow = wS.next()
                    for k in range(2):
                        b = psnext()
                        c = j2 * 2 + k
                        tk.mm(seq_group(ps[b][:], [(wv[:, kc, k * 128:(k + 1) * 128], hid[:, kc, :]) for kc in range(32)]),
                              [ow] + sum([o_hid(kc) for kc in range(32)], []), [o_ps[b]])
                        tk.emit("dve", lambda: V.tensor_tensor(out=xT[:, c, :], in0=ps[b][:], in1=xT[:, c, :], op=ALU.add),
                                [o_ps[b], o_x[c]], [o_x[c]])

        for i in range(n_tiles):
            for n in range(4):
                tk.emit("sp", lambda: SP.dma_start(out=stg[:, :], in_=x_d[i * T + n * 128:i * T + (n + 1) * 128, :]),
                        [], o_stg, dsem=s_x)
                for c4 in range(4):
                    b = psnext()
                    for cc in range(4):
                        c = c4 * 4 + cc
                        tk.tr(ps[b][:, cc * 128:(cc + 1) * 128], stg[:, c * 128:(c + 1) * 128], ident_f[:],
                              o_stg + [o_const], [o_ps[b]])
                    tk.emit("dve", lambda: V.tensor_copy(out=xT[:, c4 * 4:(c4 + 1) * 4, n * 128:(n + 1) * 128],
                                                         in_=ps[b][:].rearrange("p (k c) -> p k c", c=128)),
                            [o_ps[b]], o_x[c4 * 4:(c4 + 1) * 4])
            for l in range(depth):
                layer(i, l)
            for n in range(4):
                for c4 in range(4):
                    b = psnext()
                    for cc in range(4):
                        c = c4 * 4 + cc
                        tk.tr(ps[b][:, cc * 128:(cc + 1) * 128], xT[:, c, n * 128:(n + 1) * 128], ident_f[:],
                              [o_x[c], o_const], [o_ps[b]])
                    tk.emit("dve", lambda: V.tensor_copy(out=stg[:, c4 * 512:(c4 + 1) * 512], in_=ps[b][:]),
                            [o_ps[b]], o_stg)
                tk.emit("sp", lambda: SP.dma_start(out=y_d[i * T + n * 128:i * T + (n + 1) * 128, :], in_=stg[:, :]),
                        o_stg, [], dsem=s_x)
        SP.wait_ge(tk.sem[s_x], tk.cnt[s_x])
        print("instructions emitted:", tk.n_inst)
    return nc


def _rel_bucket_np(d):
    n = np.maximum(d, 0)
    nf = np.maximum(n, 1).astype(np.float32)
    large = 16 + (np.log(nf / np.float32(16)) / np.float32(math.log(2048 / 16)) * np.float32(16)).astype(np.int32)
    large = np.minimum(large, 31)
    return np.where(n < 16, n, large)


def host_consts():
    c = {}
    d = np.arange(LT) - 511
    bucket = _rel_bucket_np(d)
    bucket = np.where(d < 0, 32, bucket)
    oh = np.zeros((128, LT), np.float32)
    oh[bucket, np.arange(LT)] = 1.0
    c["c_oh"] = oh
    ohfar = np.zeros((128, 128), np.float32)
    ohfar[31, :] = 1.0
    c["c_ohfar"] = ohfar
    c["c_ident"] = np.eye(128, dtype=np.float32)
    c["c_jflip"] = np.ascontiguousarray(np.eye(128, dtype=np.float32)[::-1])
    es = np.zeros((32, 4096), np.float32)
    for n in range(32):
        es[n, n * 128:(n + 1) * 128] = 1.0
    c["c_esel"] = es
    c["c_tril"] = np.tril(np.ones((128, 128), np.float32))
    return c


def pack_pv(inp, depth=L):
    rows = []
    for l in range(L):
        rows += [inp["norm_mix_g"][l].reshape(16, 128), inp["norm_mlp_g"][l].reshape(16, 128),
                 inp["conv_a_w"][l].reshape(12, 128), inp["conv_b_w"][l].reshape(124, 128),
                 inp["conv_b_bias"][l].reshape(4, 128), inp["ln_b_g"][l].reshape(4, 128),
                 inp["ln_b_b"][l].reshape(4, 128), inp["ln_c_g"][l].reshape(4, 128),
                 inp["ln_c_b"][l].reshape(4, 128), inp["q_norm_g"][l].reshape(1, 128),
                 inp["k_norm_g"][l].reshape(1, 128)]
    pv = np.concatenate(rows, axis=0).astype(np.float32)
    out = np.zeros((PV_ROWS, 128), np.float32)
    out[:pv.shape[0]] = pv
    return out


def make_in_map(inp, b, s_len=SEQ):
    f = lambda a: np.ascontiguousarray(np.asarray(a, dtype=np.float32))
    m = {"x": f(inp["x"][b, :s_len]), "rel_bias": f(inp["rel_bias"]), "w_in": f(inp["w_in"]),
         "w_out_a": f(inp["w_out_a"]), "w_out_b": f(inp["w_out_b"]), "w_out_c": f(inp["w_out_c"]),
         "w_out_d": f(inp["w_out_d"]), "w_o": f(inp["w_o"]), "w_mlp_in": f(inp["w_mlp_in"]),
         "w_mlp_out": f(inp["w_mlp_out"]), "w_spatial": f(inp["w_spatial"]),
         "pv": pack_pv(inp), "bsp": f(inp["b_spatial"]).reshape(1, -1)}
    m.update(host_consts())
    return m


_NC_CACHE = {}


def kernel(**inputs):
    inp = {k: np.asarray(v) for k, v in inputs.items()}
    if "full" not in _NC_CACHE:
        _NC_CACHE["full"] = build(SEQ // T, L, SEQ)
    nc = _NC_CACHE["full"]
    real = [0, 1, 4, 5]
    zero_map = None
    in_maps = []
    for c in range(8):
        if c in real:
            in_maps.append(make_in_map(inp, real.index(c)))
        else:
            if zero_map is None:
                m0 = make_in_map(inp, 0)
                zero_map = {k: (v if k.startswith("c_") else np.zeros_like(v)) for k, v in m0.items()}
            in_maps.append(zero_map)
    res = run_bass_kernel_spmd(nc, in_maps, core_ids=list(range(8)))
    out = np.stack([np.asarray(res.results[real[b]]["y"], dtype=np.float32) for b in range(BATCH)], axis=0)
    return out
```

```python
import math
import numpy as np
from contextlib import ExitStack
import concourse.bass as bass
import concourse.mybir as mybir
from concourse.bass_utils import run_bass_kernel_spmd

F32 = mybir.dt.float32
BF16 = mybir.dt.bfloat16
AF = mybir.ActivationFunctionType
ALU = mybir.AluOpType
AX = mybir.AxisListType

D = 2048
SEQ = 8192
BATCH = 4
L = 4
T = 512
OFF_B, OFF_C, OFF_D, OFF_G, IN_COLS = 1536, 2560, 3584, 5120, 13312
DFF = 8192
EPS = 1e-6
NEG = -1e30
SCALE = 128 ** -0.5
RM = 2944
LT = 3072
PV_PER_L = 190
PV_ROWS = 768
PV_NM, PV_NF, PV_CAW, PV_CBW, PV_CBB, PV_LBG, PV_LBB, PV_LCG, PV_LCB, PV_QG, PV_KG = (
    0, 16, 32, 44, 168, 172, 176, 180, 184, 188, 189)


class Obj:
    __slots__ = ("w", "r")

    def __init__(self):
        self.w = None
        self.r = {}


class Trk:
    def __init__(self, nc, es):
        self.nc = nc
        self.es = es
        self.eng = {"pe": nc.tensor, "act": nc.scalar, "dve": nc.vector, "pool": nc.gpsimd, "sp": nc.sync}
        self.sem = {}
        self.cnt = {}
        self.isdma = {}
        self.waited = {e: {} for e in self.eng}
        for e in ("pe", "act", "dve", "pool"):
            self.newsem(e, False)
        self.n_inst = 0

    def newsem(self, key, dma=True):
        self.sem[key] = self.es.enter_context(self.nc.semaphore("s_" + key))
        self.cnt[key] = 0
        self.isdma[key] = dma
        return key

    def _wait(self, e, need):
        for key, val in need.items():
            if key == "pe" and e == "pe":
                continue
            if self.isdma[key]:
                val = self.cnt[key]
            if self.waited[e].get(key, 0) >= val:
                continue
            self.eng[e].wait_ge(self.sem[key], val)
            self.waited[e][key] = val

    def emit(self, e, fn, reads=(), writes=(), dsem=None, serial=False):
        need = {}
        for o in reads:
            if o.w is not None:
                k, v = o.w
                if need.get(k, 0) < v:
                    need[k] = v
        for o in writes:
            if o.w is not None:
                k, v = o.w
                if need.get(k, 0) < v:
                    need[k] = v
            for k, v in o.r.items():
                if need.get(k, 0) < v:
                    need[k] = v
        self._wait(e, need)
        inst = fn()
        key = dsem if dsem is not None else e
        amt = 16 if dsem is not None else 1
        self.cnt[key] += amt
        inst.then_inc(self.sem[key], amt)
        ev = (key, self.cnt[key])
        for o in reads:
            if o.r.get(key, 0) < ev[1]:
                o.r[key] = ev[1]
        for o in writes:
            o.w = ev
            o.r = {}
        self.n_inst += 1
        if serial:
            self.eng[e].wait_ge(self.sem[key], self.cnt[key])
            self.waited[e][key] = self.cnt[key]
        return ev

    def mm(self, mms, reads, writes):
        need = {}
        for o in reads:
            if o.w is not None:
                k, v = o.w
                if need.get(k, 0) < v:
                    need[k] = v
        for o in writes:
            if o.w is not None:
                k, v = o.w
                if need.get(k, 0) < v:
                    need[k] = v
            for k, v in o.r.items():
                if need.get(k, 0) < v:
                    need[k] = v
        self._wait("pe", need)
        inst = None
        for (out, lhsT, rhs, st, sp) in mms:
            inst = self.nc.tensor.matmul(out, lhsT=lhsT, rhs=rhs, start=st, stop=sp)
        self.cnt["pe"] += 1
        inst.then_inc(self.sem["pe"], 1)
        ev = ("pe", self.cnt["pe"])
        for o in reads:
            o.r["pe"] = ev[1]
        for o in writes:
            o.w = ev
            o.r = {}
        self.n_inst += len(mms)
        return ev

    def tr(self, out, in_, ident, reads, writes):
        need = {}
        for o in reads:
            if o.w is not None:
                k, v = o.w
                need[k] = max(need.get(k, 0), v)
        for o in writes:
            if o.w is not None:
                k, v = o.w
                need[k] = max(need.get(k, 0), v)
            for k, v in o.r.items():
                need[k] = max(need.get(k, 0), v)
        self._wait("pe", need)
        inst = self.nc.tensor.transpose(out, in_, ident)
        self.cnt["pe"] += 1
        inst.then_inc(self.sem["pe"], 1)
        ev = ("pe", self.cnt["pe"])
        for o in reads:
            o.r["pe"] = ev[1]
        for o in writes:
            o.w = ev
            o.r = {}
        self.n_inst += 1


def seq_group(pairs_out, lst):
    n = len(lst)
    return [(pairs_out, a, b, i == 0, i == n - 1) for i, (a, b) in enumerate(lst)]


def build(n_tiles=16, depth=L, s_len=SEQ):
    nc = bass.Bass("TRN2", target_bir_lowering=False)
    NKT = s_len // 128

    def din(name, shape):
        return nc.dram_tensor(name, list(shape), F32, kind="ExternalInput").ap()

    x_d = din("x", [s_len, D])
    relb_d = din("rel_bias", [32, 4])
    w_in_d = din("w_in", [L, D, IN_COLS])
    w_oa_d = [din("w_out_" + c, [L, 512, D]) for c in "abcd"]
    w_o_d = din("w_o", [L, D, D])
    w_1_d = din("w_mlp_in", [L, D, DFF])
    w_2_d = din("w_mlp_out", [L, DFF, D])
    wsp_d = din("w_spatial", [L, 4, 128, 128])
    pv_d = din("pv", [PV_ROWS, 128])
    bsp_d = din("bsp", [1, L * 4 * 128])
    oh_d = din("c_oh", [128, LT])
    ohfar_d = din("c_ohfar", [128, 128])
    ident_d = din("c_ident", [128, 128])
    jflip_d = din("c_jflip", [128, 128])
    esel_d = din("c_esel", [32, 4096])
    tril_d = din("c_tril", [128, 128])
    y_d = nc.dram_tensor("y", [s_len, D], F32, kind="ExternalOutput").ap()

    def dint(name, shape, dt=BF16):
        return nc.dram_tensor(name, list(shape), dt, kind="Internal").ap()

    wb_in = dint("wb_in", [L, D, IN_COLS])
    wb_oa = [dint("wb_o" + c, [L, 512, D]) for c in "abcd"]
    wb_o = dint("wb_o", [L, D, D])
    wb_1 = dint("wb_1", [L, D, DFF])
    wb_2 = dint("wb_2", [L, DFF, D])
    kc_d = dint("kcache", [L, 4, 128, s_len])
    vc_d = dint("vcache", [L, 4, 128, NKT, 128])
    ttab_d = dint("ttab", [4, LT + 128], F32)
    rtab_d = dint("rtab", [4, 128, RM])

    es = ExitStack()
    with es:
        tk = Trk(nc, es)

        def sb(name, shape, dt):
            return es.enter_context(nc.sbuf_tensor(name, list(shape), dt))

        def objs(n):
            return [Obj() for _ in range(n)]

        xT = sb("xT", [128, 16, T], F32); o_x = objs(16)
        hT = sb("hT", [128, 16, T], BF16); o_h = objs(16)
        wbuf = [sb("wbuf%d" % i, [128, 8192], BF16) for i in range(2)]; o_wb = objs(2)
        wsm = [sb("wsm%d" % i, [128, 4, 512], BF16) for i in range(2)]; o_ws = objs(2)
        Rb = [sb("Rb%d" % i, [128, RM], BF16) for i in range(2)]; o_Rb = objs(2)
        regX = sb("regX", [128, 16384], BF16); cX = objs(32)
        regY = sb("regY", [128, 8192], BF16); cY = objs(16)
        regZ = sb("regZ", [128, 8704], BF16); cZ = objs(17)
        tmp = [sb("tmp%d" % i, [128, T], F32) for i in range(4)]; o_tmp = objs(4)
        pvt = sb("pvt", [128, PV_ROWS], F32); o_pv = Obj()
        ident_f = sb("ident_f", [128, 128], F32)
        ident_b = sb("ident_b", [128, 128], BF16)
        ones_b = sb("ones_b", [128, 128], BF16)
        ones_f = sb("ones_f", [128, 128], F32)
        onesrow_b = sb("onesrow_b", [128, 128], BF16)
        jflip_b = sb("jflip_b", [128, 128], BF16)
        esel_b = sb("esel_b", [128, 4096], BF16)
        bsp_b = sb("bsp_b", [128, L * 512], BF16)
        wmT = sb("wmT", [128, L * 4, 128], BF16)
        kmean = sb("kmean", [128, L, 4, 32], F32); o_km = Obj()
        haloA = sb("haloA", [128, L, 4, 2], F32); o_hA = Obj()
        haloB = sb("haloB", [128, L, 4, 30], F32); o_hB = Obj()
        selT = sb("selT", [128, T], BF16); o_selT = Obj()
        gatebuf = sb("gatebuf", [128, 16, 32], F32); o_gb = Obj()
        selb = sb("selb", [128, 16, 32], BF16); o_selb = Obj()
        m8 = sb("m8", [128, 16, 8], F32); o_m8s = objs(16)
        thr = sb("thr", [128, 16], F32); o_thr = Obj()
        biasfar = sb("biasfar", [128, 4], F32)
        relb = sb("relb", [128, 4], F32)
        relb_s = sb("relb_s", [128, 4], F32)
        vT = sb("vT", [128, 4, 128], BF16); o_vT = Obj()
        o_const = Obj()

        ps = [es.enter_context(nc.psum_tensor("ps%d" % i, [128, 512], F32)) for i in range(8)]
        o_ps = objs(8)
        rot = [0]

        def psnext():
            i = rot[0]
            rot[0] = (i + 1) % 4
            return i

        zc = [regX[:, 0:4096].bitcast(F32).rearrange("p (k c) -> p k c", c=T),
              regX[:, 4096:8192].bitcast(F32).rearrange("p (k c) -> p k c", c=T)]

        def o_zc(z, k):
            return cX[z * 8 + 2 * k: z * 8 + 2 * k + 2]
        yin = regX[:, 8192:16384].rearrange("p (k c) -> p k c", c=T)

        def o_yin(c):
            return [cX[16 + c]]
        hid = regX[:, :].rearrange("p (k c) -> p k c", c=T)

        def o_hid(c):
            return [cX[c]]
        kbuf = [regY[:, 0:2048], regY[:, 2048:4096]]
        vbuf = [regY[:, 4096:6144].rearrange("p (k c) -> p k c", c=128),
                regY[:, 6144:8192].rearrange("p (k c) -> p k c", c=128)]

        def o_kb(s):
            return cY[4 * s:4 * s + 4]

        def o_vb(s):
            return cY[8 + 4 * s:8 + 4 * s + 4]
        mrg = regY[:, :].rearrange("p (k c) -> p k c", c=T)

        def o_mrg(c):
            return [cY[c]]
        bufA = regZ[:, 0:4112].bitcast(F32).rearrange("p (k c) -> p k c", c=514)
        o_bufA = cZ[0:9]
        bufB = regZ[:, 4112:6280].rearrange("p (k c) -> p k c", c=542)
        o_bufB = cZ[8:13]
        dg = sb("dg", [128, 16, 128], BF16); o_dg = objs(16)
        dgc = [0]
        qn_b = regZ[:, 0:2048].rearrange("p (k c) -> p k c", c=T)
        kn_b = regZ[:, 2048:4096].rearrange("p (k c) -> p k c", c=T)
        Vt = regZ[:, 4096:6144].rearrange("p (k c) -> p k c", c=T)
        PT = [regZ[:, 6144 + 512 * i:6144 + 512 * (i + 1)] for i in range(3)]
        sqb = regZ[:, 7680:8192]
        o_sqb = [cZ[15]]
        PT = PT + [sqb]
        stg = regZ[:, 0:4096].bitcast(F32)
        o_stg = cZ[0:8]

        V, ACT, PE, POOL, SP = nc.vector, nc.scalar, nc.tensor, nc.gpsimd, nc.sync

        def pvcol(l, off, idx=0):
            c = l * PV_PER_L + off + idx
            return pvt[:, c:c + 1]

        o_wsrc = {}
        cv_state = {"n": 0}
        s_cvl = [None, None]

        def convert(name, src, dst, l, rows, cols):
            key = tk.newsem("cv_%s_%d" % (name, l))
            o = Obj()
            for r0 in range(0, rows, 128):
                for c0 in range(0, cols, 4096):
                    cw = min(4096, cols - c0)
                    n = cv_state["n"]
                    cv_state["n"] += 1
                    sl = n % 2
                    if s_cvl[sl] is None:
                        s_cvl[sl] = tk.newsem("cvl%d" % sl)
                    stf = regX[:, sl * 8192:(sl + 1) * 8192].bitcast(F32)[:, 0:cw]
                    o_stf = cX[sl * 16:(sl + 1) * 16]
                    stb = regY[:, sl * 4096:sl * 4096 + cw]
                    o_stb = cY[sl * 8:(sl + 1) * 8]
                    tk.emit("sp", lambda: SP.dma_start(out=stf, in_=src[l, r0:r0 + 128, c0:c0 + cw]),
                            [], o_stf, dsem=s_cvl[sl])
                    if n % 3 == 0:
                        tk.emit("act", lambda: ACT.copy(out=stb, in_=stf), o_stf, o_stb)
                    elif n % 3 == 1:
                        tk.emit("dve", lambda: V.tensor_copy(out=stb, in_=stf), o_stf, o_stb)
                    else:
                        tk.emit("pool", lambda: POOL.tensor_copy(out=stb, in_=stf), o_stf, o_stb)
                    tk.emit("sp", lambda: SP.dma_start(out=dst[l, r0:r0 + 128, c0:c0 + cw], in_=stb),
                            o_stb, [o], dsem=key)
            o_wsrc[(name, l)] = o

        s_c = tk.newsem("cst")

        def ld(dst, src, wr):
            tk.emit("sp", lambda: SP.dma_start(out=dst, in_=src), reads=[], writes=wr, dsem=s_c, serial=True)

        epsc = sb("epsc", [128, 1], F32)
        tk.emit("dve", lambda: V.memset(epsc[:], EPS), [], [o_const])
        ld(ident_f[:], ident_d, [o_const])
        ld(tmp[0][:, 0:128], jflip_d, [o_tmp[0]])
        tk.emit("dve", lambda: V.tensor_copy(out=jflip_b[:], in_=tmp[0][:, 0:128]), [o_tmp[0]], [o_const])
        tk.emit("dve", lambda: V.tensor_copy(out=ident_b[:], in_=ident_f[:]), [o_const], [o_const])
        tk.emit("dve", lambda: V.memset(ones_b[:], 1.0), [], [o_const])
        tk.emit("dve", lambda: V.memset(ones_f[:], 1.0), [], [o_const])
        tk.emit("dve", lambda: V.memset(onesrow_b[:], 0.0), [], [o_const])
        tk.emit("dve", lambda: V.memset(onesrow_b[0:1, :], 1.0), [], [o_const])
        tk.emit("dve", lambda: V.memset(esel_b[:], 0.0), [], [o_const])
        tk.emit("dve", lambda: V.memset(bsp_b[:], 0.0), [], [o_const])
        tk.emit("dve", lambda: V.memset(selT[:], 0.0), [], [o_selT])
        tk.emit("dve", lambda: V.memset(gatebuf[:], NEG), [], [o_gb])
        tk.emit("dve", lambda: V.memset(m8[:], 0.0), [], o_m8s)
        tk.emit("dve", lambda: V.memset(selb[:], 0.0), [], [o_selb])
        tk.emit("dve", lambda: V.memset(haloA[:], 0.0), [], [o_hA])
        tk.emit("dve", lambda: V.memset(haloB[:], 0.0), [], [o_hB])
        tk.emit("dve", lambda: V.memset(relb[:], 0.0), [], [o_const])
        xs = xT[:, 0:8, :].rearrange("p k c -> p (k c)")
        ld(xs[0:32, :], esel_d, o_x[0:8])
        tk.emit("dve", lambda: V.tensor_copy(out=esel_b[0:32, :], in_=xs[0:32, :]), o_x[0:8], [o_const])
        xs2 = xT[:, 8:12, :].rearrange("p k c -> p (k c)")
        ld(xs2[0:1, :], bsp_d, o_x[8:12])
        tk.emit("dve", lambda: V.tensor_copy(out=bsp_b[0:1, :], in_=xs2[0:1, :]), o_x[8:12], [o_const])
        xs3 = xT[:, 12:14, :].rearrange("p k c -> p (k c)")
        for r in range(PV_ROWS // 128):
            half = r % 2
            o_s = o_x[12 + half]
            tk.emit("sp", lambda: SP.dma_start(out=xs3[:, half * 128:(half + 1) * 128],
                                               in_=pv_d[r * 128:(r + 1) * 128, :]),
                    [], [o_s], dsem=s_c, serial=True)
            tk.tr(ps[4][:, 0:128], xs3[:, half * 128:(half + 1) * 128], ident_f[:], [o_s, o_const], [o_ps[4]])
            tk.emit("dve", lambda: V.tensor_copy(out=pvt[:, r * 128:(r + 1) * 128], in_=ps[4][:, 0:128]),
                    [o_ps[4]], [o_pv])
        ld(tmp[1][:, 0:128], tril_d, [o_tmp[1]])
        for lg in range(depth * 4):
            l, g = lg // 4, lg % 4
            tk.emit("sp", lambda: SP.dma_start(out=tmp[2][:, 0:128], in_=wsp_d[l, g]), [], [o_tmp[2]], dsem=s_c, serial=True)
            tk.emit("dve", lambda: V.tensor_tensor(out=tmp[2][:, 128:256], in0=tmp[2][:, 0:128],
                                                   in1=tmp[1][:, 0:128], op=ALU.mult),
                    [o_tmp[2], o_tmp[1]], [o_tmp[2]])
            tk.tr(ps[4][:, 0:128], tmp[2][:, 128:256], ident_f[:], [o_tmp[2], o_const], [o_ps[4]])
            tk.emit("dve", lambda: V.tensor_copy(out=wmT[:, lg, :], in_=ps[4][:, 0:128]), [o_ps[4]], [o_const])
        ld(relb[0:32, :], relb_d, [o_const])
        tk.emit("dve", lambda: V.tensor_scalar(out=relb_s[:], in0=relb[:], scalar1=1.0 / SCALE, scalar2=None,
                                               op0=ALU.mult), [o_const], [o_const])
        tk.emit("dve", lambda: V.memset(relb_s[32:33, :], NEG), [], [o_const])
        ld(tmp[3][:, 0:128], ohfar_d, [o_tmp[3]])
        tk.mm([(ps[5][:, 0:4], tmp[3][:, 0:128], relb[:], True, True)], [o_tmp[3], o_const], [o_ps[5]])
        tk.emit("dve", lambda: V.tensor_copy(out=biasfar[:], in_=ps[5][:, 0:4]), [o_ps[5]], [o_const])
        oh_sb = xT[:, 0:6, :].rearrange("p k c -> p (k c)")
        ld(oh_sb, oh_d, o_x[0:6])
        tt_sb = xT[:, 6:12, :].rearrange("p k c -> p (k c)")
        for c in range(LT // 512):
            tk.mm([(ps[5][0:4, :], relb_s[:], oh_sb[:, c * 512:(c + 1) * 512], True, True)],
                  o_x[0:6] + [o_const], [o_ps[5]])
            tk.emit("dve", lambda: V.tensor_copy(out=tt_sb[0:4, c * 512:(c + 1) * 512], in_=ps[5][0:4, :]),
                    [o_ps[5]], o_x[6:12])
        o_ttab = Obj()
        s_t = tk.newsem("ttab")
        tk.emit("sp", lambda: SP.dma_start(out=ttab_d[:, 0:LT], in_=tt_sb[0:4, :]), o_x[6:12], [o_ttab], dsem=s_t, serial=True)
        o_rtab = Obj()
        hk_f = xT[:, 0:6, :].rearrange("p k c -> p (k c)")[:, 0:RM]
        hk_b = hT[:, 0:6, :].rearrange("p k c -> p (k c)")[:, 0:RM]
        rt_b = hT[:, 6:12, :].rearrange("p k c -> p (k c)")[:, 0:RM]
        for h in range(4):
            hank = bass.AP(tensor=ttab_d.tensor, offset=h * (LT + 128), ap=[[1, 128], [1, RM]])
            tk.emit("sp", lambda: SP.dma_start(out=hk_f, in_=hank), [o_ttab], o_x[0:6], dsem=s_t, serial=True)
            tk.emit("dve", lambda: V.tensor_copy(out=hk_b, in_=hk_f), o_x[0:6], o_h[0:6])
            c0 = 0
            while c0 < RM:
                n = min(512, RM - c0)
                tk.mm([(ps[5][:, 0:n], jflip_b[:], hk_b[:, c0:c0 + n], True, True)], o_h[0:6] + [o_const], [o_ps[5]])
                tk.emit("act", lambda: ACT.copy(out=rt_b[:, c0:c0 + n], in_=ps[5][:, 0:n]), [o_ps[5]], o_h[6:12])
                c0 += n
            tk.emit("sp", lambda: SP.dma_start(out=rtab_d[h], in_=rt_b), o_h[6:12], [o_rtab], dsem=s_t, serial=True)

        for l in range(depth):
            convert("in", w_in_d, wb_in, l, D, IN_COLS)
            for bi in range(4):
                convert("o" + "abcd"[bi], w_oa_d[bi], wb_oa[bi], l, 512, D)
            convert("o", w_o_d, wb_o, l, D, D)
            convert("1", w_1_d, wb_1, l, D, DFF)
            convert("2", w_2_d, wb_2, l, DFF, D)

        def blocks_big():
            for i in range(n_tiles):
                for l in range(depth):
                    for c in range(10):
                        yield ("in", l), wb_in[l, :, c * 512:(c + 1) * 512].rearrange("(k p) c -> p k c", p=128), 512
                    for jg in range(4):
                        for br in range(4):
                            c0 = OFF_G + br * D + jg * 512
                            yield ("in", l), wb_in[l, :, c0:c0 + 512].rearrange("(k p) c -> p k c", p=128), 512
                    for jg in range(4):
                        yield ("o", l), wb_o[l, :, jg * 512:(jg + 1) * 512].rearrange("(k p) c -> p k c", p=128), 512
                    for hh in range(2):
                        for c in range(8):
                            c0 = hh * 4096 + c * 512
                            yield ("1", l), wb_1[l, :, c0:c0 + 512].rearrange("(k p) c -> p k c", p=128), 512
                        for j2 in range(8):
                            yield ("2", l), wb_2[l, hh * 4096:(hh + 1) * 4096, j2 * 256:(j2 + 1) * 256].rearrange(
                                "(k p) c -> p k c", p=128), 256

        def blocks_small():
            for i in range(n_tiles):
                for l in range(depth):
                    for jg in range(4):
                        for br in range(4):
                            yield ("o" + "abcd"[br], l), wb_oa[br][l, :, jg * 512:(jg + 1) * 512].rearrange(
                                "(k p) c -> p k c", p=128), 512

        class WStream:
            def __init__(self, gen, bufs, obs, name):
                self.gen = gen
                self.bufs = bufs
                self.obs = obs
                self.sems = [tk.newsem("w%s%d" % (name, i)) for i in range(len(bufs))]
                self.n = 0
                self.pending = []
                self._issue()

            def _issue(self):
                try:
                    key, src, cw = next(self.gen)
                except StopIteration:
                    return
                s = self.n % len(self.bufs)
                self.n += 1
                buf = self.bufs[s]
                if len(buf.shape) == 2:
                    view = buf[:, :].rearrange("p (k c) -> p k c", c=cw)
                else:
                    view = buf[:, :, :]
                tk.emit("sp", lambda: SP.dma_start(out=view, in_=src), [o_wsrc[key]], [self.obs[s]], dsem=self.sems[s])
                self.pending.append((view, self.obs[s]))

            def next(self):
                if len(self.pending) < 2:
                    self._issue()
                v = self.pending.pop(0)
                return v

        wS = WStream(blocks_big(), wbuf, o_wb, "b")
        wT = WStream(blocks_small(), wsm, o_ws, "s")

        s_x = tk.newsem("xio")
        s_kv = tk.newsem("kvw")
        s_kl = [tk.newsem("kld%d" % i) for i in range(2)]
        s_vl = [tk.newsem("vld%d" % i) for i in range(2)]
        s_rl = [tk.newsem("rld%d" % i) for i in range(2)]
        o_kc = [Obj() for _ in range(L)]
        o_vc = [Obj() for _ in range(L)]
        rcount = [0]
        kvcount = [0]

        def rmsnorm(gofs, l):
            for c in range(16):
                tk.emit("act", lambda: ACT.activation(out=yin[:, c, :], in_=xT[:, c, :], func=AF.Square),
                        [o_x[c]], o_yin(c))
            tk.mm(seq_group(ps[4][:], [(ones_b[:], yin[:, c, :]) for c in range(16)]),
                  sum([o_yin(c) for c in range(16)], []), [o_ps[4]])
            tk.emit("act", lambda: ACT.activation(out=tmp[0][:], in_=ps[4][:], func=AF.Sqrt, bias=epsc[:, 0:1],
                                                  scale=1.0 / D), [o_ps[4], o_const], [o_tmp[0]])
            tk.emit("dve", lambda: V.reciprocal(out=tmp[0][:], in_=tmp[0][:]), [o_tmp[0]], [o_tmp[0]])
            for c in range(16):
                tk.emit("dve", lambda: V.scalar_tensor_tensor(out=hT[:, c, :], in0=xT[:, c, :],
                                                              scalar=pvcol(l, gofs, c), in1=tmp[0][:],
                                                              op0=ALU.mult, op1=ALU.mult),
                        [o_x[c], o_tmp[0], o_pv], [o_h[c]])


        def zblock(consume, tokmajor=False):
            wv, ow = wS.next()
            for k in range(4):
                b = psnext()
                if not tokmajor:
                    tk.mm(seq_group(ps[b][:], [(wv[:, kc, k * 128:(k + 1) * 128], hT[:, kc, :]) for kc in range(16)]),
                          [ow] + o_h, [o_ps[b]])
                else:
                    tk.mm(seq_group(ps[b][:], [(hT[:, kc, k * 128:(k + 1) * 128], wv[:, kc, :]) for kc in range(16)]),
                          [ow] + o_h, [o_ps[b]])
                consume(k, b)

        def ln_stats(src, o_src, sq, o_sq):
            for k in range(4):
                tk.emit("act", lambda: ACT.activation(out=sq(k), in_=src(k), func=AF.Square), o_src(k), o_sq(k))
            tk.mm(seq_group(ps[4][:], [(ones_f[:], src(k)) for k in range(4)]),
                  sum([o_src(k) for k in range(4)], []), [o_ps[4]])
            tk.mm(seq_group(ps[5][:], [(ones_f[:], sq(k)) for k in range(4)]),
                  sum([o_sq(k) for k in range(4)], []), [o_ps[5]])
            tk.emit("act", lambda: ACT.activation(out=tmp[1][:], in_=ps[4][:], func=AF.Copy, scale=1.0 / 512),
                    [o_ps[4]], [o_tmp[1]])
            tk.emit("dve", lambda: V.tensor_tensor(out=tmp[3][:], in0=tmp[1][:], in1=tmp[1][:], op=ALU.mult),
                    [o_tmp[1]], [o_tmp[3]])
            tk.emit("dve", lambda: V.scalar_tensor_tensor(out=tmp[2][:], in0=ps[5][:], scalar=1.0 / 512, in1=tmp[3][:],
                                                          op0=ALU.mult, op1=ALU.subtract),
                    [o_ps[5], o_tmp[3]], [o_tmp[2]])
            tk.emit("act", lambda: ACT.activation(out=tmp[2][:], in_=tmp[2][:], func=AF.Sqrt, bias=epsc[:, 0:1],
                                                  scale=1.0), [o_tmp[2], o_const], [o_tmp[2]])
            tk.emit("dve", lambda: V.reciprocal(out=tmp[2][:], in_=tmp[2][:]), [o_tmp[2]], [o_tmp[2]])

        def layer(i, l):
            rmsnorm(PV_NM, l)
            zblock(lambda k, b: tk.emit("act", lambda: ACT.copy(out=zc[0][:, k, :], in_=ps[b][:]), [o_ps[b]], o_zc(0, k)))
            zblock(lambda k, b: tk.emit("act", lambda: ACT.copy(out=zc[1][:, k, :], in_=ps[b][:]), [o_ps[b]], o_zc(1, k)))
            zblock(lambda k, b: tk.emit("dve", lambda: V.tensor_tensor(out=bufA[:, k, 2:514], in0=ps[b][:],
                                                                      in1=zc[1][:, k, :], op=ALU.mult),
                                        [o_ps[b]] + o_zc(1, k), o_bufA))
            tk.emit("pool", lambda: POOL.tensor_copy(out=bufA[:, :, 0:2], in_=haloA[:, l, :, :]), [o_hA], o_bufA)
            for k in range(4):
                tk.emit("dve", lambda: V.tensor_scalar(out=tmp[1][:], in0=bufA[:, k, 0:512],
                                                       scalar1=pvcol(l, PV_CAW, 0 * 4 + k), scalar2=None, op0=ALU.mult),
                        o_bufA + [o_pv], [o_tmp[1]])
                for tap in (1, 2):
                    tk.emit("dve", lambda: V.scalar_tensor_tensor(out=tmp[1][:], in0=bufA[:, k, tap:tap + 512],
                                                                  scalar=pvcol(l, PV_CAW, tap * 4 + k), in1=tmp[1][:],
                                                                  op0=ALU.mult, op1=ALU.add),
                            o_bufA + [o_tmp[1]], [o_tmp[1]])
                tk.emit("dve", lambda: V.tensor_tensor(out=yin[:, k, :], in0=tmp[1][:], in1=zc[0][:, k, :], op=ALU.mult),
                        [o_tmp[1]] + o_zc(0, k), o_yin(k))
            tk.emit("pool", lambda: POOL.tensor_copy(out=haloA[:, l, :, :], in_=bufA[:, :, 512:514]), o_bufA, [o_hA])
            zblock(lambda k, b: tk.emit("act", lambda: ACT.copy(out=zc[0][:, k, :], in_=ps[b][:]), [o_ps[b]], o_zc(0, k)))

            def cons_bg(k, b):
                tk.emit("act", lambda: ACT.activation(out=tmp[1][:], in_=ps[b][:], func=AF.Sigmoid), [o_ps[b]], [o_tmp[1]])
                tk.emit("dve", lambda: V.tensor_tensor(out=bufB[:, k, 30:542], in0=zc[0][:, k, :], in1=tmp[1][:],
                                                       op=ALU.mult), o_zc(0, k) + [o_tmp[1]], o_bufB)
            zblock(cons_bg)
            tk.emit("pool", lambda: POOL.tensor_copy(out=bufB[:, :, 0:30], in_=haloB[:, l, :, :]), [o_hB], o_bufB)
            for k in range(4):
                b = psnext()
                for g8 in range(4):
                    taps = list(range(g8 * 8, min(31, g8 * 8 + 8)))
                    slot0 = (dgc[0] % 2) * 8
                    dgc[0] += 1
                    for ti_, tap in enumerate(taps):
                        tk.emit("dve", lambda: V.tensor_scalar(out=dg[:, slot0 + ti_, :], in0=ident_b[:],
                                                               scalar1=pvcol(l, PV_CBW, tap * 4 + k), scalar2=None,
                                                               op0=ALU.mult), [o_const, o_pv], [o_dg[slot0 + ti_]])
                    tk.mm([(ps[b][:], dg[:, slot0 + ti_, :], bufB[:, k, tap:tap + 512], tap == 0, tap == 30)
                           for ti_, tap in enumerate(taps)],
                          o_bufB + [o_dg[slot0 + ti_] for ti_ in range(len(taps))], [o_ps[b]])
                tk.emit("dve", lambda: V.tensor_scalar(out=zc[1][:, k, :], in0=ps[b][:], scalar1=pvcol(l, PV_CBB, k),
                                                       scalar2=None, op0=ALU.add), [o_ps[b], o_pv], o_zc(1, k))
            tk.emit("pool", lambda: POOL.tensor_copy(out=haloB[:, l, :, :], in_=bufB[:, :, 512:542]), o_bufB, [o_hB])
            ln_stats(lambda k: zc[1][:, k, :], lambda k: o_zc(1, k), lambda k: zc[0][:, k, :], lambda k: o_zc(0, k))
            for k in range(4):
                tk.emit("dve", lambda: V.tensor_tensor(out=zc[1][:, k, :], in0=zc[1][:, k, :], in1=tmp[1][:],
                                                       op=ALU.subtract), o_zc(1, k) + [o_tmp[1]], o_zc(1, k))
                tk.emit("dve", lambda: V.tensor_tensor(out=zc[1][:, k, :], in0=zc[1][:, k, :], in1=tmp[2][:],
                                                       op=ALU.mult), o_zc(1, k) + [o_tmp[2]], o_zc(1, k))
                tk.emit("act", lambda: ACT.activation(out=yin[:, 4 + k, :], in_=zc[1][:, k, :], func=AF.Silu,
                                                      bias=pvcol(l, PV_LBB, k), scale=pvcol(l, PV_LBG, k)),
                        o_zc(1, k) + [o_pv], o_yin(4 + k))
            zblock(lambda k, b: tk.emit("act", lambda: ACT.activation(out=zc[0][:, k, :], in_=ps[b][:],
                                                                     func=AF.Gelu_apprx_tanh), [o_ps[b]], o_zc(0, k)))
            zblock(lambda k, b: tk.emit("act", lambda: ACT.activation(out=zc[1][:, k, :], in_=ps[b][:],
                                                                     func=AF.Gelu_apprx_tanh), [o_ps[b]], o_zc(1, k)))
            ln_stats(lambda k: zc[1][:, k, :], lambda k: o_zc(1, k), lambda k: bufA[:, k, 0:512], lambda k: o_bufA)
            for g in range(4):
                tk.emit("dve", lambda: V.tensor_tensor(out=zc[1][:, g, :], in0=zc[1][:, g, :], in1=tmp[1][:],
                                                       op=ALU.subtract), o_zc(1, g) + [o_tmp[1]], o_zc(1, g))
                tk.emit("dve", lambda: V.tensor_tensor(out=zc[1][:, g, :], in0=zc[1][:, g, :], in1=tmp[2][:],
                                                       op=ALU.mult), o_zc(1, g) + [o_tmp[2]], o_zc(1, g))
                tk.emit("act", lambda: ACT.activation(out=zc[1][:, g, :], in_=zc[1][:, g, :], func=AF.Identity,
                                                      bias=pvcol(l, PV_LCB, g), scale=pvcol(l, PV_LCG, g)),
                        o_zc(1, g) + [o_pv], o_zc(1, g))
                b = psnext()
                for n in range(4):
                    tk.tr(ps[b][:, n * 128:(n + 1) * 128], zc[1][:, g, n * 128:(n + 1) * 128], ident_f[:],
                          o_zc(1, g) + [o_const], [o_ps[b]])
                tk.emit("dve", lambda: V.tensor_copy(out=vT[:, :, :], in_=ps[b][:].rearrange("p (k c) -> p k c", c=128)),
                        [o_ps[b]], [o_vT])
                b2 = psnext()
                lg = l * 4 + g
                for n in range(4):
                    tk.mm([(ps[b2][:, n * 128:(n + 1) * 128], vT[:, n, :], wmT[:, lg, :], True, False),
                           (ps[b2][:, n * 128:(n + 1) * 128], onesrow_b[:], bsp_b[:, lg * 128:(lg + 1) * 128], False, True)],
                          [o_vT, o_const], [o_ps[b2]])
                tk.emit("dve", lambda: V.tensor_tensor(out=yin[:, 8 + g, :], in0=ps[b2][:], in1=zc[0][:, g, :], op=ALU.mult),
                        [o_ps[b2]] + o_zc(0, g), o_yin(8 + g))
            zblock(lambda k, b: tk.emit("act", lambda: ACT.copy(out=zc[0][:, k, :], in_=ps[b][:]), [o_ps[b]], o_zc(0, k)))
            zblock(lambda k, b: tk.emit("act", lambda: ACT.copy(out=zc[1][:, k, :], in_=ps[b][:]), [o_ps[b]], o_zc(1, k)))
            items = [(0, PV_QG, qn_b, 0, h) for h in range(4)] + [(1, PV_KG, kn_b, 4, h) for h in range(4)]
            sqbufs = [(PT[0], [cZ[12]]), (PT[1], [cZ[13]]), (PT[2], [cZ[14]]), (sqb, o_sqb)]
            nbank = {}
            for st in range(len(items) + 2):
                if st < len(items):
                    z, gofs, dstb, cbase, h = items[st]
                    sbuf_, so_ = sqbufs[st % 4]
                    tk.emit("act", lambda: ACT.activation(out=sbuf_, in_=zc[z][:, h, :], func=AF.Square), o_zc(z, h), so_)
                    bq = psnext()
                    nbank[st] = bq
                    tk.mm([(ps[bq][:], ones_b[:], sbuf_, True, True)], so_ + [o_const], [o_ps[bq]])
                if 0 <= st - 1 < len(items):
                    n1 = st - 1
                    bq = nbank[n1]
                    tk.emit("act", lambda: ACT.activation(out=tmp[n1 % 4][:], in_=ps[bq][:], func=AF.Sqrt,
                                                          bias=epsc[:, 0:1], scale=1.0 / 128),
                            [o_ps[bq], o_const], [o_tmp[n1 % 4]])
                if 0 <= st - 2 < len(items):
                    n2 = st - 2
                    z, gofs, dstb, cbase, h = items[n2]
                    tq = n2 % 4
                    tk.emit("dve", lambda: V.reciprocal(out=tmp[tq][:], in_=tmp[tq][:]), [o_tmp[tq]], [o_tmp[tq]])
                    tk.emit("dve", lambda: V.scalar_tensor_tensor(out=zc[z][:, h, :], in0=zc[z][:, h, :],
                                                                  scalar=pvcol(l, gofs), in1=tmp[tq][:],
                                                                  op0=ALU.mult, op1=ALU.mult),
                            o_zc(z, h) + [o_tmp[tq], o_pv], o_zc(z, h))
                    tk.emit("pool", lambda: POOL.tensor_copy(out=dstb[:, h, :], in_=zc[z][:, h, :]), o_zc(z, h), [cZ[cbase + h]])
            zblock(lambda k, b: tk.emit("act", lambda: ACT.copy(out=Vt[:, k, :], in_=ps[b][:]), [o_ps[b]], [cZ[8 + k]]),
                   tokmajor=True)
            for h in range(4):
                tk.emit("dve", lambda: V.tensor_reduce(out=kmean[:, l, h, 2 * i:2 * i + 2],
                                                       in_=zc[1][:, h, :].rearrange("p (b t) -> p b t", t=256),
                                                       axis=AX.X, op=ALU.add), o_zc(1, h), [o_km])
            tk.emit("sp", lambda: SP.dma_start(out=kc_d[l, :, :, i * T:(i + 1) * T].rearrange("h p t -> p h t"),
                                               in_=kn_b[:, :, :]), cZ[4:8], [o_kc[l]], dsem=s_kv)
            for h in range(4):
                tk.emit("sp", lambda: SP.dma_start(out=vc_d[l, h, :, 4 * i:4 * i + 4, :],
                                                   in_=Vt[:, :, h * 128:(h + 1) * 128]),
                        cZ[8:12], [o_vc[l]], dsem=s_kv)
            nhist = 4 * i
            nv1, nv2 = 2 * i, 2 * i + 1
            gmm = []
            for h in range(4):
                for qc in range(4):
                    nv = nv1 if qc < 2 else nv2
                    if nv == 0:
                        continue
                    idx = h * 4 + qc
                    gmm.append((ps[5][:, idx * 32:idx * 32 + nv], zc[0][:, h, qc * 128:(qc + 1) * 128],
                                kmean[:, l, h, 0:nv], True, True))
            tk.mm(gmm, sum([o_zc(0, h) for h in range(4)], []) + [o_km], [o_ps[5]])
            g4 = gatebuf[:, :, :].rearrange("p (h q) n -> p h q n", q=4)
            p4 = ps[5][:, :].rearrange("p (h q n) -> p h q n", h=4, q=4)
            if nv1 > 0:
                tk.emit("dve", lambda: V.tensor_copy(out=g4[:, :, 0:2, 0:nv1], in_=p4[:, :, 0:2, 0:nv1]), [o_ps[5]], [o_gb])
            tk.emit("dve", lambda: V.tensor_copy(out=g4[:, :, 2:4, 0:nv2], in_=p4[:, :, 2:4, 0:nv2]), [o_ps[5]], [o_gb])
            for h in range(4):
                for qc in range(4):
                    nv = nv1 if qc < 2 else nv2
                    if nv == 0:
                        continue
                    idx = h * 4 + qc
                    tk.emit("dve", lambda: V.max(out=m8[:, idx, :], in_=gatebuf[:, idx, 0:max(nv, 8)]), [o_gb], [o_m8s[idx]])
            tk.emit("dve", lambda: V.tensor_scalar(out=thr[:, 0:16], in0=m8[:, :, 2], scalar1=-1e29, scalar2=None,
                                                   op0=ALU.max), o_m8s, [o_thr])
            tk.emit("dve", lambda: V.tensor_tensor(out=selb[:, :, :], in0=gatebuf[:, :, :],
                                                   in1=thr[:, 0:16].unsqueeze(2).broadcast_to([128, 16, 32]),
                                                   op=ALU.is_ge), [o_gb, o_thr], [o_selb])
            tk.emit("dve", lambda: V.tensor_scalar(out=selb[:, :, :], in0=selb[:, :, :], scalar1=-1.0, scalar2=1e30,
                                                   op0=ALU.add, op1=ALU.mult), [o_selb], [o_selb])
            def load_rb(hh):
                rs_ = (rbase + hh) % 2
                tk.emit("sp", lambda: SP.dma_start(out=Rb[rs_][:], in_=rtab_d[hh]), [o_rtab], [o_Rb[rs_]], dsem=s_rl[rs_])
            rbase = rcount[0]
            rcount[0] += 4
            load_rb(0)
            for h in range(4):
                rs = (rbase + h) % 2
                if h + 1 < 4:
                    load_rb(h + 1)
                tk.mm([(ps[4][0:32, qc * 128:(qc + 1) * 128], selb[:, h * 4 + qc, :], ident_b[:], True, True)
                       for qc in range(4)], [o_selb, o_const], [o_ps[4]])
                tk.emit("act", lambda: ACT.copy(out=selT[0:32, :], in_=ps[4][0:32, :]), [o_ps[4]], [o_selT])
                tiles = [("h", kt) for kt in range(nhist)] + [("o", j) for j in range(4)]
                nt = len(tiles)
                nchunk = (nhist + 15) // 16
                base = kvcount[0]
                kvcount[0] += nchunk

                def load_chunk(c):
                    kvs_ = (base + c) % 2
                    kt0 = c * 16
                    nk = min(16, nhist - kt0)
                    tk.emit("sp", lambda: SP.dma_start(out=kbuf[kvs_][:, 0:nk * 128],
                                                       in_=kc_d[l, h, :, kt0 * 128:(kt0 + nk) * 128]),
                            [o_kc[l]], o_kb(kvs_), dsem=s_kl[kvs_])
                    tk.emit("sp", lambda: SP.dma_start(out=vbuf[kvs_][:, 0:nk, :],
                                                       in_=vc_d[l, h, :, kt0:kt0 + nk, :]),
                            [o_vc[l]], o_vb(kvs_), dsem=s_vl[kvs_])
                for c in range(min(2, nchunk)):
                    load_chunk(c)
                LA = 3
                pend = []
                for it in range(nt + LA):
                    if it < nt:
                        kind, kt = tiles[it]
                        b = psnext()
                        if kind == "h":
                            kvs = (base + kt // 16) % 2
                            kk = kt % 16
                            q0 = 0
                            c_off = i * T - kt * 128
                            n_blk = kt // 2
                            grp = [(ps[b][:], kbuf[kvs][:, kk * 128:(kk + 1) * 128], qn_b[:, h, :]),
                                   (ps[b][:], esel_b[:, n_blk * 128:(n_blk + 1) * 128], selT[:, :])]
                            rds = o_kb(kvs) + [cZ[h], o_selT, o_const]
                            near = c_off <= 2048
                            if near:
                                grp.append((ps[b][:], ident_b[:], Rb[rs][:, c_off + 384:c_off + 384 + 512]))
                                rds = rds + [o_Rb[rs]]
                            v_l = vbuf[kvs][:, kk, :]
                            v_o = o_vb(kvs)
                        else:
                            j = kt
                            q0 = 128 * j
                            c_off = -128 * j
                            grp = [(ps[b][:, q0:512], kn_b[:, h, j * 128:(j + 1) * 128], qn_b[:, h, q0:512])]
                            rds = [cZ[4 + h], cZ[h], o_const, o_Rb[rs]]
                            if j < 2:
                                grp.append((ps[b][:, 256:512], esel_b[:, (2 * i) * 128:(2 * i + 1) * 128], selT[:, 256:512]))
                                rds = rds + [o_selT]
                            grp.append((ps[b][:, q0:512], ident_b[:], Rb[rs][:, c_off + 384 + q0:c_off + 384 + 512]))
                            near = True
                            v_l = Vt[:, j, h * 128:(h + 1) * 128]
                            v_o = [cZ[8 + j]]
                        ng = len(grp)
                        tk.mm([(o, a, r, gi == 0, gi == ng - 1) for gi, (o, a, r) in enumerate(grp)], rds, [o_ps[b]])
                        p = it % 4
                        if near:
                            tk.emit("act", lambda: ACT.activation(out=PT[p][:, q0:512], in_=ps[b][:, q0:512], func=AF.Exp,
                                                                  scale=SCALE), [o_ps[b]], [cZ[12 + p]])
                        else:
                            tk.emit("act", lambda: ACT.activation(out=PT[p][:, q0:512], in_=ps[b][:, q0:512], func=AF.Exp,
                                                                  bias=biasfar[:, h:h + 1], scale=SCALE),
                                    [o_ps[b], o_const], [cZ[12 + p]])
                        pend.append((q0, v_l, v_o, p, it))
                    if it - LA >= 0:
                        (q0, v_l, v_o, p, ti) = pend.pop(0)
                        tk.mm([(ps[6][:, q0:512], v_l, PT[p][:, q0:512], ti == 0, ti == nt - 1),
                               (ps[7][:, q0:512], ones_b[:], PT[p][:, q0:512], ti == 0, ti == nt - 1)],
                              v_o + [cZ[12 + p], o_const], [o_ps[6], o_ps[7]])
                        kind2, kt2 = tiles[ti]
                        if kind2 == "h" and kt2 % 16 == 15 and kt2 // 16 + 2 < nchunk:
                            load_chunk(kt2 // 16 + 2)
                tk.emit("dve", lambda: V.reciprocal(out=tmp[3][:], in_=ps[7][:]), [o_ps[7]], [o_tmp[3]])
                tk.emit("dve", lambda: V.tensor_tensor(out=yin[:, 12 + h, :], in0=ps[6][:], in1=tmp[3][:], op=ALU.mult),
                        [o_ps[6], o_tmp[3]], o_yin(12 + h))
            for jg in range(4):
                for br in range(4):
                    wv, ow = wS.next()
                    wsv, ows = wT.next()
                    for k in range(4):
                        bg = psnext()
                        tk.mm(seq_group(ps[bg][:], [(wv[:, kc, k * 128:(k + 1) * 128], hT[:, kc, :]) for kc in range(16)]),
                              [ow] + o_h, [o_ps[bg]])
                        by = psnext()
                        tk.mm(seq_group(ps[by][:], [(wsv[:, kc, k * 128:(k + 1) * 128], yin[:, br * 4 + kc, :])
                                                    for kc in range(4)]),
                              [ows] + sum([o_yin(br * 4 + kc) for kc in range(4)], []), [o_ps[by]])
                        tt = 1 + (k % 2)
                        tk.emit("act", lambda: ACT.activation(out=tmp[tt][:], in_=ps[bg][:], func=AF.Sigmoid),
                                [o_ps[bg]], [o_tmp[tt]])
                        if br == 0:
                            tk.emit("dve", lambda: V.tensor_tensor(out=zc[0][:, k, :], in0=ps[by][:], in1=tmp[tt][:],
                                                                   op=ALU.mult), [o_ps[by], o_tmp[tt]], o_zc(0, k))
                        else:
                            tk.emit("dve", lambda: V.tensor_tensor(out=tmp[tt][:], in0=ps[by][:], in1=tmp[tt][:],
                                                                   op=ALU.mult), [o_ps[by], o_tmp[tt]], [o_tmp[tt]])
                            tk.emit("pool", lambda: POOL.tensor_tensor(out=zc[0][:, k, :], in0=zc[0][:, k, :],
                                                                       in1=tmp[tt][:], op=ALU.add),
                                    o_zc(0, k) + [o_tmp[tt]], o_zc(0, k))
                for k in range(4):
                    tk.emit("act", lambda: ACT.copy(out=mrg[:, jg * 4 + k, :], in_=zc[0][:, k, :]), o_zc(0, k), o_mrg(jg * 4 + k))
            for jg in range(4):
                wv, ow = wS.next()
                for k in range(4):
                    b = psnext()
                    c = jg * 4 + k
                    tk.mm(seq_group(ps[b][:], [(wv[:, kc, k * 128:(k + 1) * 128], mrg[:, kc, :]) for kc in range(16)]),
                          [ow] + sum([o_mrg(kc) for kc in range(16)], []), [o_ps[b]])
                    tk.emit("dve", lambda: V.tensor_tensor(out=xT[:, c, :], in0=ps[b][:], in1=xT[:, c, :], op=ALU.add),
                            [o_ps[b], o_x[c]], [o_x[c]])
            rmsnorm(PV_NF, l)
            for hh in range(2):
                for c8 in range(8):
                    wv, ow = wS.next()
                    for k in range(4):
                        b = psnext()
                        hc = c8 * 4 + k
                        tk.mm(seq_group(ps[b][:], [(wv[:, kc, k * 128:(k + 1) * 128], hT[:, kc, :]) for kc in range(16)]),
                              [ow] + o_h, [o_ps[b]])
                        tt = 1 + (k % 2)
                        tk.emit("act", lambda: ACT.activation(out=tmp[tt][:], in_=ps[b][:], func=AF.Relu),
                                [o_ps[b]], [o_tmp[tt]])
                        tk.emit("pool", lambda: POOL.tensor_tensor(out=hid[:, hc, :], in0=tmp[tt][:], in1=tmp[tt][:],
                                                                   op=ALU.mult), [o_tmp[tt]], o_hid(hc))
                for j2 in range(8):
                    wv, ow = wS.next()
                    for k in range(2):
                        b = psnext()
                        c = j2 * 2 + k
                        tk.mm(seq_group(ps[b][:], [(wv[:, kc, k * 128:(k + 1) * 128], hid[:, kc, :]) for kc in range(32)]),
                              [ow] + sum([o_hid(kc) for kc in range(32)], []), [o_ps[b]])
                        tk.emit("dve", lambda: V.tensor_tensor(out=xT[:, c, :], in0=ps[b][:], in1=xT[:, c, :], op=ALU.add),
                                [o_ps[b], o_x[c]], [o_x[c]])

        for i in range(n_tiles):
            for n in range(4):
                tk.emit("sp", lambda: SP.dma_start(out=stg[:, :], in_=x_d[i * T + n * 128:i * T + (n + 1) * 128, :]),
                        [], o_stg, dsem=s_x)
                for c4 in range(4):
                    b = psnext()
                    for cc in range(4):
                        c = c4 * 4 + cc
                        tk.tr(ps[b][:, cc * 128:(cc + 1) * 128], stg[:, c * 128:(c + 1) * 128], ident_f[:],
                              o_stg + [o_const], [o_ps[b]])
                    tk.emit("dve", lambda: V.tensor_copy(out=xT[:, c4 * 4:(c4 + 1) * 4, n * 128:(n + 1) * 128],
                                                         in_=ps[b][:].rearrange("p (k c) -> p k c", c=128)),
                            [o_ps[b]], o_x[c4 * 4:(c4 + 1) * 4])
            for l in range(depth):
                layer(i, l)
            for n in range(4):
                for c4 in range(4):
                    b = psnext()
                    for cc in range(4):
                        c = c4 * 4 + cc
                        tk.tr(ps[b][:, cc * 128:(cc + 1) * 128], xT[:, c, n * 128:(n + 1) * 128], ident_f[:],
                              [o_x[c], o_const], [o_ps[b]])
                    tk.emit("dve", lambda: V.tensor_copy(out=stg[:, c4 * 512:(c4 + 1) * 512], in_=ps[b][:]),
                            [o_ps[b]], o_stg)
                tk.emit("sp", lambda: SP.dma_start(out=y_d[i * T + n * 128:i * T + (n + 1) * 128, :], in_=stg[:, :]),
                        o_stg, [], dsem=s_x)
        SP.wait_ge(tk.sem[s_x], tk.cnt[s_x])
        print("instructions emitted:", tk.n_inst)
    return nc


def _rel_bucket_np(d):
    n = np.maximum(d, 0)
    nf = np.maximum(n, 1).astype(np.float32)
    large = 16 + (np.log(nf / np.float32(16)) / np.float32(math.log(2048 / 16)) * np.float32(16)).astype(np.int32)
    large = np.minimum(large, 31)
    return np.where(n < 16, n, large)


def host_consts():
    c = {}
    d = np.arange(LT) - 511
    bucket = _rel_bucket_np(d)
    bucket = np.where(d < 0, 32, bucket)
    oh = np.zeros((128, LT), np.float32)
    oh[bucket, np.arange(LT)] = 1.0
    c["c_oh"] = oh
    ohfar = np.zeros((128, 128), np.float32)
    ohfar[31, :] = 1.0
    c["c_ohfar"] = ohfar
    c["c_ident"] = np.eye(128, dtype=np.float32)
    c["c_jflip"] = np.ascontiguousarray(np.eye(128, dtype=np.float32)[::-1])
    es = np.zeros((32, 4096), np.float32)
    for n in range(32):
        es[n, n * 128:(n + 1) * 128] = 1.0
    c["c_esel"] = es
    c["c_tril"] = np.tril(np.ones((128, 128), np.float32))
    return c


def pack_pv(inp, depth=L):
    rows = []
    for l in range(L):
        rows += [inp["norm_mix_g"][l].reshape(16, 128), inp["norm_mlp_g"][l].reshape(16, 128),
                 inp["conv_a_w"][l].reshape(12, 128), inp["conv_b_w"][l].reshape(124, 128),
                 inp["conv_b_bias"][l].reshape(4, 128), inp["ln_b_g"][l].reshape(4, 128),
                 inp["ln_b_b"][l].reshape(4, 128), inp["ln_c_g"][l].reshape(4, 128),
                 inp["ln_c_b"][l].reshape(4, 128), inp["q_norm_g"][l].reshape(1, 128),
                 inp["k_norm_g"][l].reshape(1, 128)]
    pv = np.concatenate(rows, axis=0).astype(np.float32)
    out = np.zeros((PV_ROWS, 128), np.float32)
    out[:pv.shape[0]] = pv
    return out


def make_in_map(inp, b, s_len=SEQ):
    f = lambda a: np.ascontiguousarray(np.asarray(a, dtype=np.float32))
    m = {"x": f(inp["x"][b, :s_len]), "rel_bias": f(inp["rel_bias"]), "w_in": f(inp["w_in"]),
         "w_out_a": f(inp["w_out_a"]), "w_out_b": f(inp["w_out_b"]), "w_out_c": f(inp["w_out_c"]),
         "w_out_d": f(inp["w_out_d"]), "w_o": f(inp["w_o"]), "w_mlp_in": f(inp["w_mlp_in"]),
         "w_mlp_out": f(inp["w_mlp_out"]), "w_spatial": f(inp["w_spatial"]),
         "pv": pack_pv(inp), "bsp": f(inp["b_spatial"]).reshape(1, -1)}
    m.update(host_consts())
    return m


_NC_CACHE = {}


def kernel(**inputs):
    inp = {k: np.asarray(v) for k, v in inputs.items()}
    if "full" not in _NC_CACHE:
        _NC_CACHE["full"] = build(SEQ // T, L, SEQ)
    nc = _NC_CACHE["full"]
    real = [0, 1, 4, 5]
    zero_map = None
    in_maps = []
    for c in range(8):
        if c in real:
            in_maps.append(make_in_map(inp, real.index(c)))
        else:
            if zero_map is None:
                m0 = make_in_map(inp, 0)
                zero_map = {k: (v if k.startswith("c_") else np.zeros_like(v)) for k, v in m0.items()}
            in_maps.append(zero_map)
    res = run_bass_kernel_spmd(nc, in_maps, core_ids=list(range(8)))
    out = np.stack([np.asarray(res.results[real[b]]["y"], dtype=np.float32) for b in range(BATCH)], axis=0)
    return out
```
